# Optimizing a Trainium2 kernel written in Bass

```python
import math
import jax, jax.numpy as jnp
from jax import lax
import numpy as np

D_MODEL = 2048
BATCH = 8
SEQ = 2048
DEPTH = 1

GLA_HEADS = 4
GLA_DK = D_MODEL // 2 // GLA_HEADS
GLA_DV = D_MODEL // GLA_HEADS
GLA_QK_W = GLA_HEADS * GLA_DK
GLA_V_W = GLA_HEADS * GLA_DV
GLA_LOWRANK = 16
GLA_GATE_NORM = 16.0
GLA_CHUNK = 64

DIL_PATTERNS = ((128, 1), (512, 4), (2048, 16))
DIL_SLOTS = 8
DIL_HEAD_DIM = 128
DIL_W = DIL_SLOTS * DIL_HEAD_DIM
DIL_STEPS = 128
DIL_BLOCK = 128
N_DIL_HEADS = len(DIL_PATTERNS) * DIL_SLOTS

REL_BUCKETS = 32
REL_MAX_DIST = 2048

D_FF = -(-8 * D_MODEL // (3 * 256)) * 256

RMS_EPS = 1e-6
NEG_INF = -1e30

D_IN = 2 * GLA_QK_W + 2 * GLA_V_W + GLA_LOWRANK + 3 * len(DIL_PATTERNS) * DIL_W + 2 * D_MODEL

kernel_name = "hybrid_gla_dilated_gated_merge"


def _rmsnorm(x, g):
    xf = x.astype(jnp.float32)
    y = xf * lax.rsqrt(jnp.mean(xf * xf, axis=-1, keepdims=True) + RMS_EPS)
    return (y * g.astype(jnp.float32)).astype(x.dtype)


def _split_points():
    sizes = [GLA_QK_W, GLA_QK_W, GLA_V_W, GLA_V_W, GLA_LOWRANK] + [DIL_W] * (3 * len(DIL_PATTERNS)) + [D_MODEL, D_MODEL]
    return [int(v) for v in np.cumsum(sizes)[:-1]]


def _t5_bucket(dist):
    max_exact = REL_BUCKETS // 2
    d = np.maximum(dist, 1).astype(np.float64)
    large = max_exact + (np.log(d / max_exact) / math.log(REL_MAX_DIST / max_exact)
                         * (REL_BUCKETS - max_exact)).astype(np.int64)
    large = np.minimum(large, REL_BUCKETS - 1)
    return np.where(dist < max_exact, dist, large).astype(np.int32)


def _gla(q, k, v, gk, g_out, norm_w):
    B, S, H, DK = q.shape
    DV = v.shape[-1]
    C = GLA_CHUNK
    nc = S // C
    scale = DK ** -0.5

    def chunks(t):
        return t.astype(jnp.float32).reshape(B, nc, C, H, t.shape[-1])

    qc, kc, vc, gc = chunks(q), chunks(k), chunks(v), chunks(gk)
    b = jnp.cumsum(gc, axis=2)
    b_last = b[:, :, -1:]
    q_e = qc * jnp.exp(b) * scale
    k_e = kc * jnp.exp(-b)
    k_to_end = kc * jnp.exp(b_last - b)
    causal = np.tril(np.ones((C, C), dtype=bool))
    att = jnp.einsum('bnihk,bnjhk->bnhij', q_e, k_e)
    att = jnp.where(causal, att, 0.0)
    o_intra = jnp.einsum('bnhij,bnjhv->bnihv', att, vc)

    def step(state, xs):
        q_n, k_n, v_n, decay_n = xs
        o = jnp.einsum('bihk,bhkv->bihv', q_n, state)
        state = decay_n[..., None] * state + jnp.einsum('bjhk,bjhv->bhkv', k_n, v_n)
        return state, o

    xs = (jnp.moveaxis(q_e, 1, 0), jnp.moveaxis(k_to_end, 1, 0),
          jnp.moveaxis(vc, 1, 0), jnp.moveaxis(jnp.exp(b_last[:, :, 0]), 1, 0))
    state0 = jnp.zeros((B, H, DK, DV), jnp.float32)
    _, o_inter = lax.scan(step, state0, xs)
    o = (o_intra + jnp.moveaxis(o_inter, 0, 1)).reshape(B, S, H, DV)
    o = _rmsnorm(o, norm_w) * jax.nn.silu(g_out.astype(jnp.float32))
    return o.reshape(B, S, H * DV).astype(v.dtype)


def _dilated_group(q, k, v, bias_table_g, dilation):
    B, S, H, E = q.shape
    L = S // dilation
    n_blk = -(-L // DIL_BLOCK)
    Lp = n_blk * DIL_BLOCK
    scale = E ** -0.5

    def to_classes(t):
        t = t.reshape(B, L, dilation, H, E).transpose(0, 2, 3, 1, 4)
        t = jnp.pad(t, ((0, 0), (0, 0), (0, 0), (0, Lp - L), (0, 0)))
        return t.reshape(B, dilation, H, n_blk, DIL_BLOCK, E)

    def with_prev(t):
        prev = jnp.pad(t[:, :, :, :-1], ((0, 0), (0, 0), (0, 0), (1, 0), (0, 0), (0, 0)))
        return jnp.concatenate([prev, t], axis=4)

    def from_classes(t):
        e = t.shape[-1]
        t = t.reshape(B, dilation, H, Lp, e)[:, :, :, :L]
        return t.transpose(0, 3, 1, 2, 4).reshape(B, S, H, e)

    qb = to_classes(q)
    kw = with_prev(to_classes(k))
    vw = with_prev(to_classes(v))

    a_idx = np.arange(DIL_BLOCK)[:, None]
    c_idx = np.arange(2 * DIL_BLOCK)[None, :]
    steps = DIL_BLOCK + a_idx - c_idx
    in_band = (steps >= 0) & (steps <= DIL_STEPS)
    bucket = _t5_bucket(np.clip(steps, 0, None) * dilation)
    bias = jnp.transpose(bias_table_g[bucket], (2, 0, 1)).astype(jnp.float32)
    has_prev = (np.arange(n_blk) > 0)[:, None, None]
    valid = in_band[None] & ((c_idx >= DIL_BLOCK)[None] | has_prev)

    s = jnp.einsum('brhnqe,brhnke->brhnqk', qb, kw).astype(jnp.float32) * scale
    s = jnp.where(valid, s + bias[:, None], NEG_INF)
    m = jnp.max(s, axis=-1, keepdims=True)
    p = jnp.exp(s - m)
    l = jnp.sum(p, axis=-1, keepdims=True)
    num = jnp.einsum('brhnqk,brhnke->brhnqe', p, vw.astype(jnp.float32))
    return from_classes(num), from_classes(m), from_classes(l)


def _dilated_attention(parts, rel_bias):
    nums, ms, ls = [], [], []
    for gi, (window, dilation) in enumerate(DIL_PATTERNS):
        q, k, v = parts[3 * gi], parts[3 * gi + 1], parts[3 * gi + 2]
        B, S = q.shape[:2]
        shp = (B, S, DIL_SLOTS, DIL_HEAD_DIM)
        num, m, l = _dilated_group(q.reshape(shp), k.reshape(shp), v.reshape(shp),
                                   rel_bias[:, gi * DIL_SLOTS:(gi + 1) * DIL_SLOTS], dilation)
        nums.append(num); ms.append(m); ls.append(l)
    m_all = functools_max(ms)
    w = [jnp.exp(m - m_all) for m in ms]
    numer = sum(wi * ni for wi, ni in zip(w, nums))
    denom = sum(wi * li for wi, li in zip(w, ls))
    o = numer / denom
    B, S = o.shape[:2]
    return o.reshape(B, S, DIL_W)


def functools_max(arrs):
    out = arrs[0]
    for a in arrs[1:]:
        out = jnp.maximum(out, a)
    return out


def setup_inputs(seed: int = 0) -> dict:
    key = jax.random.key(seed)
    ks = jax.random.split(key, 16)
    f32 = jnp.float32

    def normal(k, shape, scale):
        return jax.random.normal(k, shape, f32) * scale

    return {
        "x": normal(ks[0], (BATCH, SEQ, D_MODEL), 1.0),
        "attn_norm": 1.0 + normal(ks[1], (DEPTH, D_MODEL), 0.01),
        "w_in": normal(ks[2], (DEPTH, D_MODEL, D_IN), D_MODEL ** -0.5),
        "w_gk_up": normal(ks[3], (DEPTH, GLA_LOWRANK, GLA_QK_W), GLA_LOWRANK ** -0.5),
        "b_gk": normal(ks[4], (DEPTH, GLA_QK_W), 0.01),
        "gla_norm": 1.0 + normal(ks[5], (DEPTH, GLA_DV), 0.01),
        "b_gate": normal(ks[6], (DEPTH, 2 * D_MODEL), 0.01),
        "w_branch_gla": normal(ks[7], (DEPTH, GLA_V_W, D_MODEL), GLA_V_W ** -0.5),
        "w_branch_dil": normal(ks[8], (DEPTH, DIL_W, D_MODEL), DIL_W ** -0.5),
        "w_out": normal(ks[9], (DEPTH, D_MODEL, D_MODEL), D_MODEL ** -0.5),
        "ffn_norm": 1.0 + normal(ks[10], (DEPTH, D_MODEL), 0.01),
        "w_ffn_in": normal(ks[11], (DEPTH, D_MODEL, 2 * D_FF), D_MODEL ** -0.5),
        "w_ffn_out": normal(ks[12], (DEPTH, D_FF, D_MODEL), D_FF ** -0.5),
        "rel_bias": normal(ks[13], (REL_BUCKETS, N_DIL_HEADS), 0.5),
        "final_norm": 1.0 + normal(ks[14], (D_MODEL,), 0.01),
    }


def reference(x, attn_norm, w_in, w_gk_up, b_gk, gla_norm, b_gate, w_branch_gla, w_branch_dil,
              w_out, ffn_norm, w_ffn_in, w_ffn_out, rel_bias, final_norm):
    B, S = x.shape[:2]
    split_points = _split_points()
    n_dil = 3 * len(DIL_PATTERNS)
    for l in range(DEPTH):
        h = _rmsnorm(x, attn_norm[l])
        z = h @ w_in[l]
        parts = jnp.split(z, split_points, axis=-1)
        q_a, k_a, v_a, g_a, lr = parts[:5]
        dil_parts = parts[5:5 + n_dil]
        gate_a_logit, gate_b_logit = parts[5 + n_dil], parts[6 + n_dil]

        gk = jax.nn.log_sigmoid((lr @ w_gk_up[l] + b_gk[l]).astype(jnp.float32)) / GLA_GATE_NORM
        shp_k = (B, S, GLA_HEADS, GLA_DK)
        shp_v = (B, S, GLA_HEADS, GLA_DV)
        o_gla = _gla(q_a.reshape(shp_k), k_a.reshape(shp_k), v_a.reshape(shp_v),
                     gk.reshape(shp_k), g_a.reshape(shp_v), gla_norm[l])
        o_dil = _dilated_attention(dil_parts, rel_bias).astype(x.dtype)

        gate_a = jax.nn.sigmoid(gate_a_logit + b_gate[l, :D_MODEL])
        gate_b = jax.nn.sigmoid(gate_b_logit + b_gate[l, D_MODEL:])
        merged = gate_a * (o_gla @ w_branch_gla[l]) + gate_b * (o_dil @ w_branch_dil[l])
        x = x + merged @ w_out[l]

        hf = _rmsnorm(x, ffn_norm[l])
        gu = hf @ w_ffn_in[l]
        gate, up = gu[..., :D_FF], gu[..., D_FF:]
        x = x + (jax.nn.silu(gate) * up) @ w_ffn_out[l]
    return _rmsnorm(x, final_norm)
```

```python
import math
from contextlib import ExitStack

import numpy as np
import concourse.bass as bass
import concourse.mybir as mybir
from concourse.bass_utils import run_bass_kernel_spmd

F32 = mybir.dt.float32
BF16 = mybir.dt.bfloat16
AF = mybir.ActivationFunctionType
ALU = mybir.AluOpType

T = 2048
D = 2048
KC = D // 128
DFF = 5632
FC = DFF // 128
DIN = 19472
C_Q, C_K, C_V, C_G, C_LR, C_DIL, C_GA = 0, 1024, 2048, 4096, 6144, 6160, 15376
EPS = 1e-6
NEG = -30000.0
DILS = (1, 4, 16)

ARENA_BYTES = 207 * 1024 + 512
PAGE = 512


def _dtsize(dt):
    return 4 if dt in (F32, mybir.dt.float32r, mybir.dt.int32, mybir.dt.uint32) else 2


class _Op:
    __slots__ = ("eng", "fn", "dma", "deps", "signal", "sigsem", "sigval", "idx", "semkey", "group")


class Sched:
    ENGS = ("pe", "act", "dve", "pool", "sp")
    SEM_ROLL = 30000

    def __init__(self, nc):
        self.nc = nc
        self.ops = []
        self.last_writer = {}
        self.readers = {}

    @staticmethod
    def keys_of(x):
        if isinstance(x, (tuple, str)):
            return [x]
        esz = _dtsize(x.dtype)
        lo = int(x.offset) * esz
        span = esz
        for (step, cnt) in x.ap[1:]:
            span += (cnt - 1) * abs(step) * esz
        hi = lo + span - 1
        nm = x.name
        if nm == "psum":
            return [("psum", p) for p in range(lo // 2048, hi // 2048 + 1)]
        return [(nm, p) for p in range(lo // PAGE, hi // PAGE + 1)]

    def op(self, eng, fn, reads=(), writes=(), dma=False, semkey=None, group=None):
        o = _Op()
        o.eng, o.fn, o.dma, o.semkey, o.group = eng, fn, dma, semkey, group
        o.signal = False
        o.sigsem = None
        o.sigval = None
        o.idx = len(self.ops)
        rk = []
        wk = []
        for r in reads:
            for k_ in self.keys_of(r):
                (wk if k_[0] == "psum" else rk).append(k_)
        for w in writes:
            wk.extend(self.keys_of(w))
        deps = {}
        lw = self.last_writer
        rd = self.readers
        for r in rk:
            w = lw.get(r)
            if w is not None:
                deps[w.idx] = w
        for k in wk:
            w = lw.get(k)
            if w is not None:
                deps[w.idx] = w
            for r_ in rd.get(k, ()):
                deps[r_.idx] = r_
        o.deps = list(deps.values())
        for r in rk:
            l = rd.get(r)
            if l is None:
                rd[r] = [o]
            elif not l or l[-1] is not o:
                l.append(o)
        for k in wk:
            lw[k] = o
            rd[k] = []
        self.ops.append(o)
        return o

    def dma(self, eng, fn, reads=(), writes=(), semkey=None, group=None):
        assert semkey is not None
        return self.op(eng, fn, reads, writes, dma=True, semkey=semkey, group=group)

    def emit(self, stack):
        nc = self.nc
        for o in self.ops:
            for d in o.deps:
                if d.dma:
                    continue
                if d.eng == o.eng and o.eng == "pe" and not o.dma:
                    continue
                d.signal = True
        cur_sem, cur_cnt = {}, {}
        nsem = [0]

        def new_sem(tag):
            nsem[0] += 1
            return stack.enter_context(nc.semaphore(f"s{tag}{nsem[0]}"))

        dsem, dcnt = {}, {}
        groups = {}
        for o in self.ops:
            if o.dma:
                k = o.semkey
                if k not in dsem or (dcnt[k] > self.SEM_ROLL and (o.group is None or o.group not in groups)):
                    dsem[k] = new_sem("d")
                    dcnt[k] = 0
                dcnt[k] += 16
                o.sigsem = dsem[k]
                o.sigval = dcnt[k]
                if o.group is not None:
                    groups.setdefault(o.group, []).append(o)
            elif o.signal:
                e = o.eng
                if e not in cur_sem or cur_cnt[e] >= self.SEM_ROLL:
                    cur_sem[e] = new_sem(e)
                    cur_cnt[e] = 0
                cur_cnt[e] += 1
                o.sigsem = cur_sem[e]
                o.sigval = cur_cnt[e]
        for g, members in groups.items():
            mx = max(m.sigval for m in members)
            for m in members:
                assert m.sigsem is members[0].sigsem
                m.sigval = mx
        self.nsem = nsem[0]
        by_eng = {e: [] for e in self.ENGS}
        for o in self.ops:
            by_eng[o.eng].append(o)

        def run_queue(eh, ops):
            waited = {}
            for o in ops:
                need = {}
                for d in o.deps:
                    if (not d.dma) and d.eng == o.eng and o.eng == "pe" and not o.dma:
                        continue
                    k = id(d.sigsem)
                    if k not in need or need[k][1] < d.sigval:
                        need[k] = (d.sigsem, d.sigval)
                for k, (sem, val) in need.items():
                    if waited.get(k, 0) >= val:
                        continue
                    eh.wait_ge(sem, val)
                    waited[k] = val
                if o.fn is None:
                    continue
                ins = o.fn(eh)
                if o.dma:
                    ins.then_inc(o.sigsem, 16)
                elif o.signal:
                    ins.then_inc(o.sigsem, 1)

        block = stack.enter_context(nc.Block())

        @block.tensor
        def _(e):
            run_queue(e, by_eng["pe"])

        @block.scalar
        def _(e):
            run_queue(e, by_eng["act"])

        @block.vector
        def _(e):
            run_queue(e, by_eng["dve"])

        @block.gpsimd
        def _(e):
            run_queue(e, by_eng["pool"])

        @block.sync
        def _(e):
            run_queue(e, by_eng["sp"])


def build(stages=99, debug=False):
    rec = []
    _build(stages, debug, None, rec)
    return _build(stages, debug, rec, [])


def _build(stages, debug, plan_in, rec):
    nc = bass.Bass("TRN2", target_bir_lowering=False)
    okind = "ExternalOutput" if debug else "Internal"

    def din(name, shape):
        return nc.dram_tensor(name, list(shape), F32, kind="ExternalInput").ap()

    x_d = din("x", [T, D])
    w_in_d = din("w_in", [D, DIN])
    w_a_d = din("w_a", [D, D])
    w_b_d = din("w_b", [1024, D])
    w_out_d = din("w_out", [D, D])
    w_f1_d = din("w_f1", [D, 2 * DFF])
    w_f2_d = din("w_f2", [DFF, D])
    norms_d = din("norms", [128, 3 * D + 512])
    consts_d = din("consts", [128, 896])
    wgk_d = din("wgk", [17, 1024])
    bgate_d = din("bgate", [128, 32])
    btab_d = din("btab", [128, 24, 256])
    out_d = nc.dram_tensor("out", [T, D], F32, kind="ExternalOutput").ap()

    gates_d = nc.dram_tensor("gates_s", [32, 128, T], BF16, kind=okind).ap()
    oglaT_d = nc.dram_tensor("oglaT_s", [16, 128, T], BF16, kind=okind).ap()
    odilT_d = nc.dram_tensor("odilT_s", [8, 128, T], BF16, kind=okind).ap()
    x1_d = nc.dram_tensor("x1_s", [T, D], F32, kind=okind).ap()
    y_d = nc.dram_tensor("y_s", [16, 128, T // 2], F32).ap()

    st = ExitStack()
    with st:
        S = Sched(nc)
        arena = st.enter_context(nc.sbuf_tensor("arena", [128, ARENA_BYTES // 4], F32))
        psum = st.enter_context(nc.psum_tensor("psum", [128, 4096], F32))

        class Bump:
            def __init__(self, lo, hi):
                self.lo, self.hi, self.cur = lo, hi, lo

            def alloc(self, shape, dt):
                esz = _dtsize(dt)
                n = int(np.prod(shape[1:]))
                nb = (n * esz + PAGE - 1) // PAGE * PAGE
                off = self.cur
                assert off + nb <= self.hi, f"arena overflow {off + nb} > {self.hi}"
                self.cur += nb
                a = arena[0:shape[0], off // 4:(off + n * esz + 3) // 4]
                if dt != F32:
                    a = a.bitcast(dt)
                if len(shape) == 3:
                    a = a.rearrange("p (a b) -> p a b", b=shape[2])
                elif len(shape) == 4:
                    a = a.rearrange("p (a b c) -> p a b c", b=shape[2], c=shape[3])
                return a

            def mark(self):
                return self.cur

            def reset(self, m):
                self.cur = m

        B = Bump(0, ARENA_BYTES)

        def pbank(b, dt=F32, cols=None, off=0):
            a = psum[:, b * 512:(b + 1) * 512]
            if dt != F32:
                a = a.bitcast(dt)
            if cols is not None:
                a = a[:, off:off + cols]
            return a

        def mm(out, lhsT, rhs, start=True, stop=True):
            S.op("pe", lambda e: e.matmul(out, lhsT=lhsT, rhs=rhs, start=start, stop=stop),
                 reads=[lhsT, rhs], writes=[out])

        def tr(out, in_, ident):
            S.op("pe", lambda e: e.transpose(out, in_, ident), reads=[in_, ident], writes=[out])

        def act(out, in_, func, bias=None, scale=None, accum=None, extra_reads=()):
            kw = {}
            rd = [in_] + list(extra_reads)
            if bias is not None:
                kw["bias"] = bias
                if not isinstance(bias, (int, float)):
                    rd.append(bias)
            if scale is not None:
                kw["scale"] = scale
                if not isinstance(scale, (int, float)):
                    rd.append(scale)
            wr = [out]
            if accum is not None:
                kw["accum_out"] = accum
                wr.append(accum)
            S.op("act", lambda e: e.activation(out=out, in_=in_, func=func, **kw), reads=rd, writes=wr)

        def vcopy(eng, out, in_):
            if eng == "act":
                S.op("act", lambda e: e.copy(out, in_), reads=[in_], writes=[out])
            else:
                S.op(eng, lambda e: e.tensor_copy(out, in_), reads=[in_], writes=[out])

        def tt(eng, out, in0, in1, op):
            S.op(eng, lambda e: e.tensor_tensor(out, in0, in1, op), reads=[in0, in1], writes=[out])

        def ts(eng, out, in0, s1, s2, op0, op1=None):
            rd = [in0] + [s for s in (s1, s2) if s is not None and not isinstance(s, (int, float))]
            if op1 is None:
                S.op(eng, lambda e: e.tensor_scalar(out, in0, s1, s2, op0), reads=rd, writes=[out])
            else:
                S.op(eng, lambda e: e.tensor_scalar(out, in0, s1, s2, op0, op1), reads=rd, writes=[out])

        def stt(eng, out, in0, scalar, in1, op0, op1):
            rd = [in0, in1] + ([] if isinstance(scalar, (int, float)) else [scalar])
            S.op(eng, lambda e: e.scalar_tensor_tensor(out, in0, scalar, in1, op0, op1), reads=rd, writes=[out])

        def recip_lp(out, in_):
            def fn(e):
                with nc.allow_low_precision(reason="fp32 reciprocal, result stored as a bf16 matmul operand"):
                    return e.reciprocal(out, in_)
            S.op("dve", fn, reads=[in_], writes=[out])

        def memset(eng, ap, val):
            S.op(eng, lambda e: e.memset(ap, val), writes=[ap])

        def dma_sp(out, in_, semkey, reads=(), writes=(), group=None):
            S.dma("sp", lambda e: e.dma_start(out=out, in_=in_), reads=reads, writes=writes, semkey=semkey, group=group)

        cst = B.alloc([128, 896], F32)
        ident32 = cst[:, 0:128]
        causal32 = cst[:, 128:256]
        ucum32 = cst[:, 256:384]
        maskadd = cst[:, 512:768]
        neghalf = cst[:, 768:769]
        dma_sp(cst, consts_d, "ld_cst", writes=[cst])
        cbf = B.alloc([128, 256], BF16)
        ident16 = cbf[:, 0:128]
        ones16 = cbf[:, 128:256]
        vcopy("dve", ident16, ident32)
        vcopy("dve", ones16, cst[:, 384:512])
        gnw = B.alloc([128, 512], F32)
        dma_sp(gnw, norms_d[:, 3 * D:3 * D + 512], "ld_gnw", writes=[gnw])
        bgate = B.alloc([128, 32], F32)
        dma_sp(bgate, bgate_d, "ld_bg", writes=[bgate])
        small = B.alloc([128, 64], F32)
        small_i = [0]

        def stat_col():
            i = small_i[0] % 64
            small_i[0] += 1
            return small[:, i:i + 1]

        NSLOT = 3
        SLOT_BYTES = 16384
        PREFETCH = 2
        wslot_raw = [B.alloc([128, SLOT_BYTES // 2], BF16) for _ in range(NSLOT)]
        plan = plan_in if plan_in is not None else []
        issued = []
        cursor = [0]

        def _views(i, spec):
            si = i % NSLOT
            views = []
            o = 0
            for (wname, kc, c, w) in spec:
                views.append(wslot_raw[si][:, o:o + kc * w].rearrange("p (k n) -> p k n", n=w))
                o += kc * w
            assert o * 2 <= SLOT_BYTES
            return views

        def _issue(i):
            spec = plan[i]
            si = i % NSLOT
            views = _views(i, spec)
            grp = ("wf", i)
            for (wname, kc, c, w), view in zip(spec, views):
                wv = WD[wname].rearrange("(k p) n -> p k n", p=128)
                src = wv[:, :, c:c + w]
                S.dma("pool", (lambda e, dst=view, src=src: e.dma_start(out=dst, in_=src)),
                      writes=[view], semkey=("wslot", si), group=grp)
            issued.append(views)

        def take(spec):
            i = cursor[0]
            cursor[0] += 1
            rec.append(spec)
            if plan_in is None:
                return _views(i, spec)
            assert plan[i] == spec, (i, plan[i], spec)

            def ndesc(sp):
                return sum(kc * 8 for (_, kc, _, _) in sp)

            while len(issued) < min(len(plan), i + 1 + PREFETCH):
                j = len(issued)
                if j > i and sum(ndesc(plan[t]) for t in range(i, j + 1)) > 800:
                    break
                _issue(j)
            return issued[i]

        class BankPool:
            def __init__(self, ids):
                self.ids = list(ids)
                self.i = 0

            def one(self):
                b_ = self.ids[self.i % len(self.ids)]
                self.i += 1
                return b_

            def group(self, n):
                ng = len(self.ids) // n
                g_ = self.i % ng
                self.i += 1
                return self.ids[g_ * n]

        ALLB = BankPool(range(8))
        LOB = BankPool(range(4))
        HIB = BankPool(range(4, 8))

        def nbank():
            return ALLB.one()

        def ngroup(n):
            return ALLB.group(n)

        def g_gemm_ws(wview, kc, nm, rhs_fn, ntq, epilogue, m0=0, mrows=128, bp=None, split=False):
            bp = bp or ALLB
            for m in range(nm):
                if split:
                    passes = [[0, 1], [2, 3]]
                else:
                    passes = [list(range(ntq))]
                for tqs in passes:
                    base = bp.group(len(tqs) if len(tqs) in (2, 4) else 4)
                    pss = [None] * ntq
                    for i_, tq in enumerate(tqs):
                        pss[tq] = pbank(base + i_)[0:mrows, :]
                    for k in range(kc):
                        for tq in tqs:
                            mm(pss[tq], wview[:, k, m * 128:m * 128 + mrows], rhs_fn(k, tq), start=(k == 0), stop=(k == kc - 1))
                        yield
                    epilogue(m0 + m, pss)
                    yield

        def run(g):
            for _ in g:
                pass

        def gemm_ws(*a_, **kw):
            run(g_gemm_ws(*a_, **kw))

        def interleave(ga, gb, ra=1, rb=1):
            da = db = False
            while not (da and db):
                if not da:
                    for _ in range(ra):
                        try:
                            next(ga)
                        except StopIteration:
                            da = True
                            break
                if not db:
                    for _ in range(rb):
                        try:
                            next(gb)
                        except StopIteration:
                            db = True
                            break

        def limited(g, n):
            for _ in range(n):
                try:
                    next(g)
                except StopIteration:
                    return
                yield

        evr = [0]

        def ev_eng():
            evr[0] += 1
            return "act" if evr[0] % 2 else "dve"

        WD = {"w_in": w_in_d, "w_a": w_a_d, "w_b": w_b_d, "w_out": w_out_d, "w_f1": w_f1_d, "w_f2": w_f2_d}

        def spec_gate(p):
            return [("w_in", KC, C_GA + p * 256, 256)]

        def spec_lr():
            return [("w_in", KC, C_LR, 16)]

        def spec_qk(h):
            return [("w_in", KC, C_Q + h * 256, 256), ("w_in", KC, C_K + h * 256, 256)]

        def spec_vg(h, which):
            return [("w_in", KC, (C_V if which == 0 else C_G) + h * 512, 512)]

        def spec_dil(j, g):
            return [("w_in", KC, C_DIL + (3 * g + t_) * 1024 + j * 128, 128) for t_ in range(3)]

        def spec_ab(p):
            return [("w_a", KC, p * 256, 256), ("w_b", 8, p * 256, 256)]

        def spec_out(p):
            return [("w_out", KC, p * 512, 512)]

        def spec_f1(p):
            return [("w_f1", KC, p * 256, 256), ("w_f1", KC, DFF + p * 256, 256)]

        def spec_f2(m):
            return [("w_f2", FC, m * 128, 128)]

        hT_off = B.mark()
        hT = B.alloc([128, KC, T], BF16)
        phase_mark = B.mark()
        hT_end = phase_mark

        def rms_stat(rows, width):
            ss = stat_col()
            act(rms_junk[:, 0:width], rows, AF.Square, accum=ss)
            rs = stat_col()
            act(rs, ss, AF.Ln, bias=EPS, scale=1.0 / width)
            act(rs, rs, AF.Exp, scale=-0.5)
            return rs

        def rms_rows(rows, width, gmul, out):
            rs = rms_stat(rows, width)
            stt("dve", out, rows, rs, gmul, ALU.mult, ALU.mult)

        def xpose_rows(h_, dstT, tb, eng=None):
            for half in range(2):
                pv = pbank(nbank(), BF16)
                for c in range(8):
                    cc = half * 8 + c
                    tr(pv[:, c * 128:(c + 1) * 128], h_[:, cc * 128:(cc + 1) * 128], ident16)
                vcopy(eng or ev_eng(), dstT[:, half * 8:half * 8 + 8, tb * 128:(tb + 1) * 128],
                      pv.rearrange("p (c t) -> p c t", t=128))

        def row_pipeline(ntb, Lf, Tf, Rf, Xf):
            if Lf:
                Lf(0)
            for i in range(ntb + 2):
                if Lf and i + 1 < ntb:
                    Lf(i + 1)
                if Tf and i < ntb:
                    Tf(i)
                if Rf and 0 <= i - 1 < ntb:
                    Rf(i - 1)
                if Xf and 0 <= i - 2 < ntb:
                    Xf(i - 2)

        rms_junk = B.alloc([128, D], BF16)
        gbc = B.alloc([128, D], F32)
        dma_sp(gbc, norms_d[:, 0:D], "ld_gbc", writes=[gbc])
        NX = 6
        NH0 = 3
        xt = [B.alloc([128, D], F32) for _ in range(NX)]
        hn0 = [B.alloc([128, D], BF16) for _ in range(NH0)]

        def x_load(tb):
            t_ = xt[tb % NX]
            dma_sp(t_, x_d[tb * 128:(tb + 1) * 128, :], ("ld_x", tb % NX), writes=[t_])

        for tb0 in range(NX - 2):
            x_load(tb0)
        row_pipeline(T // 128, None, (lambda tb: x_load(tb + NX - 2) if tb + NX - 2 < T // 128 else None),
                     lambda tb: rms_rows(xt[tb % NX], D, gbc, hn0[tb % NH0]),
                     lambda tb: xpose_rows(hn0[tb % NH0], hT, tb))
        B.reset(phase_mark)

        def hT_rhs(k, tq):
            return hT[:, k, tq * 512:(tq + 1) * 512]

        G = None
        if stages >= 1:
            nbg = B.alloc([128, 32], F32)
            ts("dve", nbg, bgate, -1.0, None, ALU.mult)
            gt = [B.alloc([128, 1024], BF16) for _ in range(2)]
            gtmp = [B.alloc([128, 512], F32) for _ in range(2)]
            gi = [0]

            def gate_epi(m, pss):
                for hf in range(2):
                    if pss[hf * 2] is None:
                        continue
                    t_ = gt[gi[0] % 2]
                    for q2 in range(2):
                        tq = hf * 2 + q2
                        tm = gtmp[q2]
                        act(t_[:, q2 * 512:(q2 + 1) * 512], pss[tq], AF.Exp, bias=nbg[:, m:m + 1], scale=-1.0)
                    dma_sp(gates_d[m][:, hf * 1024:(hf + 1) * 1024], t_, ("st_gt", gi[0] % 2), reads=[t_],
                           writes=[("gates", m, hf)])
                    gi[0] += 1

            def g_gates():
                for p in range(16):
                    (wv_,) = take(spec_gate(p))
                    yield from g_gemm_ws(wv_, KC, 2, hT_rhs, 4, gate_epi, m0=p * 2, bp=LOB, split=True)

            G = g_gates()
            if stages < 2:
                run(G)
            phase_mark = B.mark()

        if stages >= 2:
            scale_k = 256 ** -0.5
            lrT = B.alloc([17, T], F32)
            memset("dve", lrT, 1.0)
            (wv,) = take(spec_lr())

            def lr_epi(m, pss):
                for tq in range(4):
                    vcopy(ev_eng(), lrT[0:16, tq * 512:(tq + 1) * 512], pss[tq])

            gemm_ws(wv, KC, 1, hT_rhs, 4, lr_epi, mrows=16)
            gla_mark = B.mark()

            for h in range(4):
                B.reset(gla_mark)
                wgk = B.alloc([17, 256], F32)
                dma_sp(wgk, wgk_d[:, h * 256:(h + 1) * 256], "ld_wgk", writes=[wgk])
                qeT = B.alloc([128, 2, T], BF16)
                keT = B.alloc([128, 2, T], BF16)
                v_h = B.alloc([128, 16, 512], BF16)
                bT_off = B.mark()
                bT = B.alloc([128, 2, T], F32)
                dec = B.alloc([128, 2, 16], F32)
                st32 = B.alloc([128, 2, 512], F32)
                st16 = B.alloc([128, 2, 512], BF16)
                tmp_mark = B.mark()
                B.reset(bT_off)
                sg_h = B.alloc([128, 16, 512], BF16)
                B.reset(tmp_mark)
                gkr = [B.alloc([128, 256], F32) for _ in range(2)]
                e1 = [B.alloc([128, 256], F32) for _ in range(2)]
                B.reset(tmp_mark)
                tmp512 = [B.alloc([128, 512], F32) for _ in range(2)]
                attb = [B.alloc([128, 128], BF16) for _ in range(2)]
                onb = [B.alloc([128, 512], BF16) for _ in range(2)]
                sdt = [B.alloc([128, 512], F32) for _ in range(2)]
                ket = [B.alloc([128, 256], BF16) for _ in range(2)]
                ogc = [B.alloc([128, 4, 128], BF16) for _ in range(2)]
                rms_junk = B.alloc([128, 512], BF16)

                def g_decay():
                    pp = BankPool([6, 7])
                    bkA, bkB = 4, 5

                    def stA(tb):
                        pu = pbank(pp.one(), F32, 256)
                        mm(pu, lrT[0:17, tb * 128:(tb + 1) * 128], wgk[0:17, :])
                        a1 = e1[tb % 2]
                        act(a1, pu, AF.Exp, scale=-1.0)
                        act(gkr[tb % 2], a1, AF.Ln, bias=1.0)

                    def stB(tb):
                        t4 = tb % 4
                        g1 = gkr[tb % 2]
                        for kc, bk in ((0, bkA), (1, bkB)):
                            mm(pbank(bk, F32, 128, t4 * 128), g1[:, kc * 128:(kc + 1) * 128], ucum32)
                        if t4 == 3:
                            tbg = tb // 4
                            vcopy("dve", bT[:, 0, tbg * 512:(tbg + 1) * 512], pbank(bkA))
                            vcopy("dve", bT[:, 1, tbg * 512:(tbg + 1) * 512], pbank(bkB))

                    stA(0)
                    yield
                    for tb in range(16):
                        if tb + 1 < 16:
                            stA(tb + 1)
                            yield
                        stB(tb)
                        yield

                if G is not None:
                    interleave(g_decay(), limited(G, 68), 1, 2)
                else:
                    run(g_decay())
                act(dec, bT.rearrange("p k (c t) -> p k c t", t=128)[:, :, :, 127], AF.Exp)

                wq, wk = take(spec_qk(h))
                ti = [0]

                def mk_qk_epi(isq):
                    def epi(m, pss):
                        kc = m
                        for tq in range(4):
                            tm = tmp512[ti[0] % 2]
                            ti[0] += 1
                            sl = slice(tq * 512, (tq + 1) * 512)
                            if isq:
                                act(tm, bT[:, kc, sl], AF.Exp)
                                stt("dve", qeT[:, kc, sl], pss[tq], scale_k, tm, ALU.mult, ALU.mult)
                            else:
                                act(tm, bT[:, kc, sl], AF.Exp, scale=-1.0)
                                tt("dve", keT[:, kc, sl], pss[tq], tm, ALU.mult)
                    return epi

                gemm_ws(wq, KC, 2, hT_rhs, 4, mk_qk_epi(True))
                gemm_ws(wk, KC, 2, hT_rhs, 4, mk_qk_epi(False))

                for which in range(2):
                    (wv,) = take(spec_vg(h, which))
                    for tbg in range(4):
                        base = ngroup(4)
                        for k in range(KC):
                            for t4 in range(4):
                                tb = tbg * 4 + t4
                                mm(pbank(base + t4), hT[:, k, tb * 128:(tb + 1) * 128], wv[:, k, :], start=(k == 0), stop=(k == KC - 1))
                        for t4 in range(4):
                            tb = tbg * 4 + t4
                            if which == 0:
                                vcopy(ev_eng(), v_h[:, tb, :], pbank(base + t4))
                            else:
                                tm = tmp512[ti[0] % 2]
                                ti[0] += 1
                                act(tm, pbank(base + t4), AF.Silu)
                                tt("dve", sg_h[:, tb, :], tm, gnw, ALU.mult)

                memset("dve", st32, 0.0)
                memset("dve", st16, 0.0)

                def g_rec(h=h, qeT=qeT, keT=keT, v_h=v_h, sg_h=sg_h, st32=st32, st16=st16, dec=dec, attb=attb, onb=onb,
                          sdt=sdt, ket=ket, ogc=ogc):
                    bS, bO, bK = 4, 5, (6, 7)

                    def att_mask(c):
                        csl = slice(c * 128, (c + 1) * 128)
                        pa = pbank(bS, F32, 128, 0)
                        for kc in range(2):
                            mm(pa, keT[:, kc, csl], qeT[:, kc, csl], start=(kc == 0), stop=(kc == 1))
                        tt("dve", attb[c % 2], pa, causal32, ALU.mult)

                    def k_T(c):
                        csl = slice(c * 128, (c + 1) * 128)
                        pvk = pbank(bS, BF16, 256, 256)
                        for kc in range(2):
                            tr(pvk[:, kc * 128:(kc + 1) * 128], keT[:, kc, csl], ident16)
                        vcopy("act", ket[c % 2], pvk)

                    def out_T(c):
                        csl = slice(c * 128, (c + 1) * 128)
                        ob = onb[c % 2]
                        pv = pbank(bS, BF16, 512, 512)
                        for j in range(4):
                            tr(pv[:, j * 128:(j + 1) * 128], ob[:, j * 128:(j + 1) * 128], ident16)
                        oc = ogc[c % 2]
                        vcopy("act", oc, pv.rearrange("p (j t) -> p j t", t=128))
                        dma_sp(oglaT_d[h * 4:(h + 1) * 4, :, csl].rearrange("c p t -> p c t"), oc, ("st_og", c % 2),
                               reads=[oc], writes=[("oglaT", h, c)])

                    att_mask(0)
                    k_T(0)
                    yield
                    for c in range(16):
                        csl = slice(c * 128, (c + 1) * 128)
                        po = pbank(bO)
                        mm(po, attb[c % 2], v_h[:, c, :], start=True, stop=False)
                        for kc in range(2):
                            mm(po, qeT[:, kc, csl], st16[:, kc, :], start=False, stop=(kc == 1))
                        rs = rms_stat(po, 512)
                        yield
                        if c < 15:
                            kt = ket[c % 2]
                            pks = []
                            for kc in range(2):
                                pk = pbank(bK[kc])
                                mm(pk, kt[:, kc * 128:(kc + 1) * 128], v_h[:, c, :])
                                pks.append(pk)
                            att_mask(c + 1)
                            for kc in range(2):
                                sd = sdt[kc]
                                act(sd, st32[:, kc, :], AF.Copy, scale=dec[:, kc, c:c + 1])
                                stt("dve", st16[:, kc, :], pks[kc], dec[:, kc, c:c + 1], sd, ALU.mult, ALU.add)
                            for kc in range(2):
                                stt("dve", st32[:, kc, :], pks[kc], dec[:, kc, c:c + 1], sdt[kc], ALU.mult, ALU.add)
                            yield
                        stt("dve", onb[c % 2], po, rs, sg_h[:, c, :], ALU.mult, ALU.mult)
                        if c + 1 < 15:
                            k_T(c + 1)
                        yield
                        if c >= 1:
                            out_T(c - 1)
                            yield
                    out_T(15)
                    yield

                if G is not None:
                    interleave(g_rec(), limited(G, 204), 1, 3)
                else:
                    run(g_rec())
            if G is not None:
                run(G)
            phase_mark = hT_end
            B.reset(phase_mark)

        if stages >= 3:
            scale_a = 128 ** -0.5
            B.reset(phase_mark)
            XT = [[B.alloc([128, T], BF16) for _ in range(3)] for _ in range(3)]
            vtok = [B.alloc([128, 16, 128], BF16) for _ in range(3)]
            acc = B.alloc([128, 2, T], F32)
            bm = B.alloc([128, 3, 256], F32)
            NR = 4
            stt_ = [B.alloc([128, 256], F32) for _ in range(NR)]
            ptl = [B.alloc([128, 256], BF16) for _ in range(NR)]
            odT = B.alloc([128, T], BF16)
            items = [(j, g) for j in range(8) for g in range(3)]

            def g_P(j, g):
                d = DILS[g]
                dma_sp(bm[:, g, :], btab_d[:, g * 8 + j, :], ("ld_bm", g), writes=[bm[:, g, :]])
                tt("dve", bm[:, g, :], bm[:, g, :], maskadd, ALU.add)
                wvs = take(spec_dil(j, g))
                for t_ in range(3):
                    def perm_epi(m, pss, g=g, d=d, t_=t_):
                        dst = XT[g][t_]
                        for tq in range(4):
                            if pss[tq] is None:
                                continue
                            if d == 1:
                                vcopy(ev_eng(), dst[:, tq * 512:(tq + 1) * 512], pss[tq])
                            else:
                                n = 512 // d
                                dv = dst.rearrange("p (r m) -> p r m", r=d)[:, :, tq * n:(tq + 1) * n]
                                vcopy(ev_eng(), dv, pss[tq].rearrange("p (m r) -> p r m", r=d))
                    yield from g_gemm_ws(wvs[t_], KC, 1, hT_rhs, 4, perm_epi, bp=LOB, split=True)
                for half in range(2):
                    pv = pbank(LOB.one(), BF16)
                    for b8 in range(8):
                        blk = half * 8 + b8
                        tr(pv[:, b8 * 128:(b8 + 1) * 128], XT[g][2][:, blk * 128:(blk + 1) * 128], ident16)
                    vcopy(ev_eng(), vtok[g][:, half * 8:(half + 1) * 8, :], pv.rearrange("p (b e) -> p b e", e=128))
                    yield

            uctr = [0]

            def g_A(j, g):
                d = DILS[g]
                L = T // d
                nblk = L // 128
                units = [(r, n) for r in range(d) for n in range(nblk)]

                def qk(u):
                    r, n = units[u]
                    blk = r * nblk + n
                    qb = XT[g][0][:, blk * 128:(blk + 1) * 128]
                    bks = HIB.one()
                    ps_s = pbank(bks, F32, 256)
                    parts = ([0] if n > 0 else []) + [1]
                    for x_ in parts:
                        kb = blk - 1 + x_
                        mm(ps_s[:, x_ * 128:(x_ + 1) * 128], XT[g][1][:, kb * 128:(kb + 1) * 128], qb)
                    lo = parts[0] * 128
                    sb = stt_[uctr[0] % NR]
                    pt = ptl[uctr[0] % NR]
                    uctr[0] += 1
                    stt("dve", sb[:, lo:256], ps_s[:, lo:256], scale_a, bm[:, g, lo:256], ALU.mult, ALU.add)
                    act(pt[:, lo:256], sb[:, lo:256], AF.Exp)
                    return (bks, parts, blk, pt, r, n)

                def pv_(ctx):
                    bks, parts, blk, pt, r, n = ctx
                    ps_n = pbank(bks, F32, 256, 256)
                    for ii, x_ in enumerate(parts):
                        kb = blk - 1 + x_
                        mm(ps_n[:, 0:128], vtok[g][:, kb, :], pt[:, x_ * 128:(x_ + 1) * 128],
                           start=(ii == 0), stop=(ii == len(parts) - 1))
                    for ii, x_ in enumerate(parts):
                        mm(ps_n[:, 128:256], ones16, pt[:, x_ * 128:(x_ + 1) * 128],
                           start=(ii == 0), stop=(ii == len(parts) - 1))
                    av = acc.rearrange("p x (m r) -> p x m r", r=d)[:, :, n * 128:(n + 1) * 128, r]
                    pn = ps_n.rearrange("p (x q) -> p x q", q=128)
                    if g == 0:
                        vcopy("act", av, pn)
                    else:
                        tt("dve", av, pn, av, ALU.add)

                LA = 2
                nu = len(units)
                ctxs = {}
                for u0 in range(min(LA, nu)):
                    ctxs[u0] = qk(u0)
                    yield
                for u in range(nu):
                    if u + LA < nu:
                        ctxs[u + LA] = qk(u + LA)
                        yield
                    pv_(ctxs.pop(u))
                    yield
                if g == 2:
                    act(acc[:, 1, :], acc[:, 1, :], AF.Ln)
                    act(acc[:, 1, :], acc[:, 1, :], AF.Exp, scale=-1.0)
                    tt("dve", odT, acc[:, 0, :], acc[:, 1, :], ALU.mult)
                    dma_sp(odilT_d[j], odT, "st_odT", reads=[odT], writes=[("odilT", j)])
                    yield

            run(g_P(*items[0]))
            for i, it in enumerate(items):
                if i + 1 < len(items):
                    interleave(g_A(*it), g_P(*items[i + 1]), 1, 3)
                else:
                    run(g_A(*it))
            B.reset(phase_mark)

        TH = T // 2
        post_mark = hT_off

        NRS = 3

        def rows_load(tb, res_d, r0, ybl, xrl):
            xr = xrl[tb % NRS]
            if ybl is not None:
                yb = ybl[tb % NRS]
                dma_sp(yb, y_d[:, :, tb * 128:(tb + 1) * 128].rearrange("c p t -> p c t"), ("ld_yb", tb % NRS),
                       reads=[("y", m) for m in range(16)], writes=[yb])
            dma_sp(xr, res_d[r0 + tb * 128:r0 + (tb + 1) * 128, :], ("ld_xr", tb % NRS),
                   reads=[("x1", r0 // TH, tb)] if res_d is x1_d else [], writes=[xr])

        def rows_add(tb, ybl, xrl, ysb=None):
            xr = xrl[tb % NRS]
            for cg in range(4):
                pv = pbank(nbank())
                for i in range(4):
                    src = ysb[:, cg * 4 + i, tb * 128:(tb + 1) * 128] if ysb is not None else ybl[tb % NRS][:, cg * 4 + i, :]
                    tr(pv[:, i * 128:(i + 1) * 128], src, ident32)
                tt("dve", xr[:, cg * 512:(cg + 1) * 512], pv, xr[:, cg * 512:(cg + 1) * 512], ALU.add)
            return xr

        if stages >= 4:
            for h2 in range(2):
                B.reset(post_mark)
                tsl = slice(h2 * TH, (h2 + 1) * TH)
                mgT = B.alloc([128, KC, TH], BF16)
                s1_mark = B.mark()
                ogh = B.alloc([128, 16, TH], BF16)
                odh = B.alloc([128, 8, TH], BF16)
                gab = [B.alloc([128, 2, TH], BF16) for _ in range(2)]
                gfl = [B.alloc([128, 2, TH], F32) for _ in range(2)]
                t1 = [B.alloc([128, 512], F32) for _ in range(2)]
                t2 = [B.alloc([128, 512], F32) for _ in range(2)]
                dma_sp(ogh, oglaT_d[:, :, tsl].rearrange("c p t -> p c t"), "ld_ogh",
                       reads=[("oglaT", i, c) for i in range(4) for c in range(16)], writes=[ogh])
                dma_sp(odh, odilT_d[:, :, tsl].rearrange("c p t -> p c t"), "ld_odh",
                       reads=[("odilT", i) for i in range(8)], writes=[odh])
                for p in range(8):
                    wa, wb = take(spec_ab(p))
                    for mi in range(2):
                        m = p * 2 + mi
                        gb_ = gab[m % 2]
                        dma_sp(gb_[:, 0, :], gates_d[m][:, tsl], ("ld_ga", m % 2), reads=[("gates", m, h2)], writes=[gb_[:, 0, :]])
                        dma_sp(gb_[:, 1, :], gates_d[16 + m][:, tsl], ("ld_gb", m % 2), reads=[("gates", 16 + m, h2)], writes=[gb_[:, 1, :]])
                        gf_ = gfl[m % 2]
                        act(gf_, gb_, AF.Ln, bias=1.0)
                        act(gf_, gf_, AF.Exp, scale=-1.0)
                        base = ngroup(4)
                        pA = [pbank(base + tq) for tq in range(2)]
                        pB = [pbank(base + 2 + tq) for tq in range(2)]
                        for k in range(KC):
                            for tq in range(2):
                                mm(pA[tq], wa[:, k, mi * 128:(mi + 1) * 128], ogh[:, k, tq * 512:(tq + 1) * 512],
                                   start=(k == 0), stop=(k == KC - 1))
                        for k in range(8):
                            for tq in range(2):
                                mm(pB[tq], wb[:, k, mi * 128:(mi + 1) * 128], odh[:, k, tq * 512:(tq + 1) * 512],
                                   start=(k == 0), stop=(k == 7))
                        for tq in range(2):
                            sl = slice(tq * 512, (tq + 1) * 512)
                            tt("dve", t1[tq], pA[tq], gf_[:, 0, sl], ALU.mult)
                            tt("dve", t2[tq], pB[tq], gf_[:, 1, sl], ALU.mult)
                            tt("dve", mgT[:, m, sl], t1[tq], t2[tq], ALU.add)
                B.reset(s1_mark)
                yall = B.alloc([128, 16, TH], F32)

                def y_sb_epi(m, pss):
                    for tq in range(2):
                        vcopy(ev_eng(), yall[:, m, tq * 512:(tq + 1) * 512], pss[tq])

                yT = None
                yi = [0]

                def y_epi(m, pss):
                    y = yT[yi[0] % 2]
                    for tq in range(2):
                        vcopy(ev_eng(), y[:, tq * 512:(tq + 1) * 512], pss[tq])
                    dma_sp(y_d[m], y, ("st_y", yi[0] % 2), reads=[y], writes=[("y", m)])
                    yi[0] += 1

                for p in range(4):
                    (wo,) = take(spec_out(p))
                    gemm_ws(wo, KC, 4, lambda k, tq: mgT[:, k, tq * 512:(tq + 1) * 512], 2, y_sb_epi, m0=p * 4)
                s2b_mark = B.mark()
                B.reset(post_mark)
                hfT = B.alloc([128, KC, TH], BF16)
                B.reset(s2b_mark)
                xrl = [B.alloc([128, D], F32) for _ in range(NRS)]
                gbc2 = B.alloc([128, D], F32)
                hn2 = [B.alloc([128, D], BF16) for _ in range(2)]
                rms_junk = B.alloc([128, D], BF16)
                dma_sp(gbc2, norms_d[:, D:2 * D], "ld_gbc2", writes=[gbc2])

                def s2b_T(tb):
                    xr = rows_add(tb, None, xrl, ysb=yall)
                    dma_sp(x1_d[h2 * TH + tb * 128:h2 * TH + (tb + 1) * 128, :], xr, ("st_x1", tb % NRS), reads=[xr],
                           writes=[("x1", h2, tb)])

                row_pipeline(8, lambda tb: rows_load(tb, x_d, h2 * TH, None, xrl), s2b_T,
                             (lambda tb: rms_rows(xrl[tb % NRS], D, gbc2, hn2[tb % 2])) if stages >= 5 else None,
                             (lambda tb: xpose_rows(hn2[tb % 2], hfT, tb, eng="act")) if stages >= 5 else None)
                if stages >= 5:
                    B.reset(s1_mark)
                    actT = B.alloc([128, FC, TH], BF16)
                    ft = [B.alloc([128, 512], F32) for _ in range(2)]
                    fi = [0]
                    for p in range(22):
                        wg_, wu_ = take(spec_f1(p))
                        for fi_ in range(2):
                            f = p * 2 + fi_
                            base = ngroup(4)
                            pG = [pbank(base + tq) for tq in range(2)]
                            pU = [pbank(base + 2 + tq) for tq in range(2)]
                            for k in range(KC):
                                for tq in range(2):
                                    mm(pG[tq], wg_[:, k, fi_ * 128:(fi_ + 1) * 128], hfT[:, k, tq * 512:(tq + 1) * 512],
                                       start=(k == 0), stop=(k == KC - 1))
                            for k in range(KC):
                                for tq in range(2):
                                    mm(pU[tq], wu_[:, k, fi_ * 128:(fi_ + 1) * 128], hfT[:, k, tq * 512:(tq + 1) * 512],
                                       start=(k == 0), stop=(k == KC - 1))
                            for tq in range(2):
                                tm = ft[fi[0] % 2]
                                fi[0] += 1
                                act(tm, pG[tq], AF.Silu)
                                tt("dve", actT[:, f, tq * 512:(tq + 1) * 512], tm, pU[tq], ALU.mult)
                    yT = [B.alloc([128, TH], F32) for _ in range(2)]
                    for m in range(16):
                        (wv,) = take(spec_f2(m))
                        gemm_ws(wv, FC, 1, lambda k, tq: actT[:, k, tq * 512:(tq + 1) * 512], 2, y_epi, m0=m)
                    B.reset(post_mark)
                    ybl = [B.alloc([128, 16, 128], F32) for _ in range(NRS)]
                    xrl = [B.alloc([128, D], F32) for _ in range(NRS)]
                    gbc3 = B.alloc([128, D], F32)
                    rms_junk = B.alloc([128, D], BF16)
                    orow = [B.alloc([128, D], F32) for _ in range(2)]
                    dma_sp(gbc3, norms_d[:, 2 * D:3 * D], "ld_gbc3", writes=[gbc3])
                    def s4b_R(tb):
                        o_ = orow[tb % 2]
                        rms_rows(xrl[tb % NRS], D, gbc3, o_)
                        r0 = h2 * TH + tb * 128
                        dma_sp(out_d[r0:r0 + 128, :], o_, ("st_out", tb % 2), reads=[o_], writes=[("out", h2, tb)])

                    row_pipeline(8, lambda tb: rows_load(tb, x1_d, h2 * TH, ybl, xrl),
                                 lambda tb: rows_add(tb, ybl, xrl), s4b_R, None)

        final_reads = []
        for k in list(S.last_writer.keys()):
            if isinstance(k, tuple) and k[0] in ("out", "x1", "oglaT", "odilT", "gates", "y"):
                final_reads.append(k)
        S.op("sp", None, reads=final_reads)
        assert plan_in is None or cursor[0] == len(plan), (cursor[0], len(plan))
        S.emit(st)
        print(f"[build] ops={len(S.ops)} sems={S.nsem}")
    return nc


def _t5_bucket(dist):
    max_exact = 16
    d = np.maximum(dist, 1).astype(np.float64)
    large = max_exact + (np.log(d / max_exact) / math.log(2048 / max_exact) * (32 - max_exact)).astype(np.int64)
    large = np.minimum(large, 31)
    return np.where(dist < max_exact, dist, large).astype(np.int64)


def _host_consts():
    c = np.zeros((128, 896), np.float32)
    j = np.arange(128)[:, None]
    i = np.arange(128)[None, :]
    c[:, 0:128] = np.eye(128, dtype=np.float32)
    c[:, 128:256] = (j <= i)
    c[:, 256:384] = np.where(j <= i, -1.0 / 16.0, 0.0)
    c[:, 384:512] = 1.0
    c[:, 512:640] = np.where(j >= i, 0.0, NEG)
    c[:, 640:768] = np.where(j <= i, 0.0, NEG)
    c[:, 768:896] = -0.5
    return c


def _bias_index():
    cidx = np.arange(128)[:, None]
    a = np.arange(128)[None, :]
    idx = np.zeros((3, 128, 2, 128), np.int64)
    for g, d in enumerate(DILS):
        steps_prev = 128 + a - cidx
        steps_cur = a - cidx
        idx[g, :, 0, :] = _t5_bucket(np.clip(steps_prev, 0, None) * d)
        idx[g, :, 1, :] = _t5_bucket(np.clip(steps_cur, 0, None) * d)
    return idx


_NC_CACHE = {}


def make_in_maps(inputs):
    f = lambda a: np.ascontiguousarray(np.asarray(a, dtype=np.float32))
    x = f(inputs["x"])
    norms = np.concatenate([f(inputs["attn_norm"])[0], f(inputs["ffn_norm"])[0], f(inputs["final_norm"]),
                            f(inputs["gla_norm"])[0]])
    norms = np.ascontiguousarray(np.broadcast_to(norms[None, :], (128, norms.shape[0])))
    wgk = np.ascontiguousarray(np.concatenate([f(inputs["w_gk_up"])[0], f(inputs["b_gk"])[0][None, :]], axis=0))
    bgate = np.ascontiguousarray(f(inputs["b_gate"])[0].reshape(32, 128).T)
    rb = f(inputs["rel_bias"])
    idx = _bias_index()
    btab = np.zeros((128, 24, 256), np.float32)
    for g in range(3):
        for s in range(8):
            hh = g * 8 + s
            btab[:, hh, :] = rb[:, hh][idx[g]].reshape(128, 256)
    shared = {
        "w_in": f(inputs["w_in"])[0], "w_a": f(inputs["w_branch_gla"])[0], "w_b": f(inputs["w_branch_dil"])[0],
        "w_out": f(inputs["w_out"])[0], "w_f1": f(inputs["w_ffn_in"])[0], "w_f2": f(inputs["w_ffn_out"])[0],
        "norms": norms, "consts": _host_consts(), "wgk": wgk, "bgate": bgate, "btab": btab,
    }
    return [dict(shared, x=np.ascontiguousarray(x[b])) for b in range(x.shape[0])]


def kernel(**inputs):
    in_maps = make_in_maps(inputs)
    if "nc" not in _NC_CACHE:
        _NC_CACHE["nc"] = build()
    nc = _NC_CACHE["nc"]
    res = run_bass_kernel_spmd(nc, in_maps, core_ids=list(range(len(in_maps))))
    out = np.stack([np.asarray(r["out"], dtype=np.float32) for r in res.results], axis=0)
    return out
```

```python
import math
from contextlib import ExitStack

import numpy as np
import concourse.bass as bass
import concourse.mybir as mybir
from concourse.bass_utils import run_bass_kernel_spmd

F32 = mybir.dt.float32
BF16 = mybir.dt.bfloat16
AF = mybir.ActivationFunctionType
ALU = mybir.AluOpType

T = 2048
D = 2048
KC = D // 128
DFF = 5632
FC = DFF // 128
DIN = 19472
C_Q, C_K, C_V, C_G, C_LR, C_DIL, C_GA = 0, 1024, 2048, 4096, 6144, 6160, 15376
EPS = 1e-6
NEG = -30000.0
DILS = (1, 4, 16)

ARENA_BYTES = 207 * 1024 + 512
PAGE = 512


def _dtsize(dt):
    return 4 if dt in (F32, mybir.dt.float32r, mybir.dt.int32, mybir.dt.uint32) else 2


class _Op:
    __slots__ = ("eng", "fn", "dma", "deps", "signal", "sigsem", "sigval", "idx", "semkey", "group")


class Sched:
    ENGS = ("pe", "act", "dve", "pool", "sp")
    SEM_ROLL = 30000

    def __init__(self, nc):
        self.nc = nc
        self.ops = []
        self.last_writer = {}
        self.readers = {}

    @staticmethod
    def keys_of(x):
        if isinstance(x, (tuple, str)):
            return [x]
        esz = _dtsize(x.dtype)
        lo = int(x.offset) * esz
        span = esz
        for (step, cnt) in x.ap[1:]:
            span += (cnt - 1) * abs(step) * esz
        hi = lo + span - 1
        nm = x.name
        if nm == "psum":
            return [("psum", p) for p in range(lo // 2048, hi // 2048 + 1)]
        return [(nm, p) for p in range(lo // PAGE, hi // PAGE + 1)]

    def op(self, eng, fn, reads=(), writes=(), dma=False, semkey=None, group=None):
        o = _Op()
        o.eng, o.fn, o.dma, o.semkey, o.group = eng, fn, dma, semkey, group
        o.signal = False
        o.sigsem = None
        o.sigval = None
        o.idx = len(self.ops)
        rk = []
        wk = []
        for r in reads:
            for k_ in self.keys_of(r):
                (wk if k_[0] == "psum" else rk).append(k_)
        for w in writes:
            wk.extend(self.keys_of(w))
        deps = {}
        lw = self.last_writer
        rd = self.readers
        for r in rk:
            w = lw.get(r)
            if w is not None:
                deps[w.idx] = w
        for k in wk:
            w = lw.get(k)
            if w is not None:
                deps[w.idx] = w
            for r_ in rd.get(k, ()):
                deps[r_.idx] = r_
        o.deps = list(deps.values())
        for r in rk:
            l = rd.get(r)
            if l is None:
                rd[r] = [o]
            elif not l or l[-1] is not o:
                l.append(o)
        for k in wk:
            lw[k] = o
            rd[k] = []
        self.ops.append(o)
        return o

    def dma(self, eng, fn, reads=(), writes=(), semkey=None, group=None):
        assert semkey is not None
        return self.op(eng, fn, reads, writes, dma=True, semkey=semkey, group=group)

    def emit(self, stack):
        nc = self.nc
        for o in self.ops:
            for d in o.deps:
                if d.dma:
                    continue
                if d.eng == o.eng and o.eng == "pe" and not o.dma:
                    continue
                d.signal = True
        cur_sem, cur_cnt = {}, {}
        nsem = [0]

        def new_sem(tag):
            nsem[0] += 1
            return stack.enter_context(nc.semaphore(f"s{tag}{nsem[0]}"))

        dsem, dcnt = {}, {}
        groups = {}
        for o in self.ops:
            if o.dma:
                k = o.semkey
                if k not in dsem or (dcnt[k] > self.SEM_ROLL and (o.group is None or o.group not in groups)):
                    dsem[k] = new_sem("d")
                    dcnt[k] = 0
                dcnt[k] += 16
                o.sigsem = dsem[k]
                o.sigval = dcnt[k]
                if o.group is not None:
                    groups.setdefault(o.group, []).append(o)
            elif o.signal:
                e = o.eng
                if e not in cur_sem or cur_cnt[e] >= self.SEM_ROLL:
                    cur_sem[e] = new_sem(e)
                    cur_cnt[e] = 0
                cur_cnt[e] += 1
                o.sigsem = cur_sem[e]
                o.sigval = cur_cnt[e]
        for g, members in groups.items():
            mx = max(m.sigval for m in members)
            for m in members:
                assert m.sigsem is members[0].sigsem
                m.sigval = mx
        self.nsem = nsem[0]
        by_eng = {e: [] for e in self.ENGS}
        for o in self.ops:
            by_eng[o.eng].append(o)

        def run_queue(eh, ops):
            waited = {}
            for o in ops:
                need = {}
                for d in o.deps:
                    if (not d.dma) and d.eng == o.eng and o.eng == "pe" and not o.dma:
                        continue
                    k = id(d.sigsem)
                    if k not in need or need[k][1] < d.sigval:
                        need[k] = (d.sigsem, d.sigval)
                for k, (sem, val) in need.items():
                    if waited.get(k, 0) >= val:
                        continue
                    eh.wait_ge(sem, val)
                    waited[k] = val
                if o.fn is None:
                    continue
                ins = o.fn(eh)
                if o.dma:
                    ins.then_inc(o.sigsem, 16)
                elif o.signal:
                    ins.then_inc(o.sigsem, 1)

        block = stack.enter_context(nc.Block())

        @block.tensor
        def _(e):
            run_queue(e, by_eng["pe"])

        @block.scalar
        def _(e):
            run_queue(e, by_eng["act"])

        @block.vector
        def _(e):
            run_queue(e, by_eng["dve"])

        @block.gpsimd
        def _(e):
            run_queue(e, by_eng["pool"])

        @block.sync
        def _(e):
            run_queue(e, by_eng["sp"])


def build(stages=99, debug=False):
    rec = []
    _build(stages, debug, None, rec)
    return _build(stages, debug, rec, [])


def _build(stages, debug, plan_in, rec):
    nc = bass.Bass("TRN2", target_bir_lowering=False)
    okind = "ExternalOutput" if debug else "Internal"

    def din(name, shape):
        return nc.dram_tensor(name, list(shape), F32, kind="ExternalInput").ap()

    x_d = din("x", [T, D])
    w_in_d = din("w_in", [D, DIN])
    w_a_d = din("w_a", [D, D])
    w_b_d = din("w_b", [1024, D])
    w_out_d = din("w_out", [D, D])
    w_f1_d = din("w_f1", [D, 2 * DFF])
    w_f2_d = din("w_f2", [DFF, D])
    norms_d = din("norms", [128, 3 * D + 512])
    consts_d = din("consts", [128, 896])
    wgk_d = din("wgk", [17, 1024])
    bgate_d = din("bgate", [128, 32])
    btab_d = din("btab", [128, 24, 256])
    out_d = nc.dram_tensor("out", [T, D], F32, kind="ExternalOutput").ap()

    gates_d = nc.dram_tensor("gates_s", [32, 128, T], BF16, kind=okind).ap()
    oglaT_d = nc.dram_tensor("oglaT_s", [16, 128, T], BF16, kind=okind).ap()
    odilT_d = nc.dram_tensor("odilT_s", [8, 128, T], BF16, kind=okind).ap()
    x1_d = nc.dram_tensor("x1_s", [T, D], F32, kind=okind).ap()
    y_d = nc.dram_tensor("y_s", [16, 128, T // 2], F32).ap()

    st = ExitStack()
    with st:
        S = Sched(nc)
        arena = st.enter_context(nc.sbuf_tensor("arena", [128, ARENA_BYTES // 4], F32))
        psum = st.enter_context(nc.psum_tensor("psum", [128, 4096], F32))

        class Bump:
            def __init__(self, lo, hi):
                self.lo, self.hi, self.cur = lo, hi, lo

            def alloc(self, shape, dt):
                esz = _dtsize(dt)
                n = int(np.prod(shape[1:]))
                nb = (n * esz + PAGE - 1) // PAGE * PAGE
                off = self.cur
                assert off + nb <= self.hi, f"arena overflow {off + nb} > {self.hi}"
                self.cur += nb
                a = arena[0:shape[0], off // 4:(off + n * esz + 3) // 4]
                if dt != F32:
                    a = a.bitcast(dt)
                if len(shape) == 3:
                    a = a.rearrange("p (a b) -> p a b", b=shape[2])
                elif len(shape) == 4:
                    a = a.rearrange("p (a b c) -> p a b c", b=shape[2], c=shape[3])
                return a

            def mark(self):
                return self.cur

            def reset(self, m):
                self.cur = m

        B = Bump(0, ARENA_BYTES)

        def pbank(b, dt=F32, cols=None, off=0):
            a = psum[:, b * 512:(b + 1) * 512]
            if dt != F32:
                a = a.bitcast(dt)
            if cols is not None:
                a = a[:, off:off + cols]
            return a

        def mm(out, lhsT, rhs, start=True, stop=True):
            S.op("pe", lambda e: e.matmul(out, lhsT=lhsT, rhs=rhs, start=start, stop=stop),
                 reads=[lhsT, rhs], writes=[out])

        def tr(out, in_, ident):
            S.op("pe", lambda e: e.transpose(out, in_, ident), reads=[in_, ident], writes=[out])

        def act(out, in_, func, bias=None, scale=None, accum=None, extra_reads=()):
            kw = {}
            rd = [in_] + list(extra_reads)
            if bias is not None:
                kw["bias"] = bias
                if not isinstance(bias, (int, float)):
                    rd.append(bias)
            if scale is not None:
                kw["scale"] = scale
                if not isinstance(scale, (int, float)):
                    rd.append(scale)
            wr = [out]
            if accum is not None:
                kw["accum_out"] = accum
                wr.append(accum)
            S.op("act", lambda e: e.activation(out=out, in_=in_, func=func, **kw), reads=rd, writes=wr)

        def vcopy(eng, out, in_):
            if eng == "act":
                S.op("act", lambda e: e.copy(out, in_), reads=[in_], writes=[out])
            else:
                S.op(eng, lambda e: e.tensor_copy(out, in_), reads=[in_], writes=[out])

        def tt(eng, out, in0, in1, op):
            S.op(eng, lambda e: e.tensor_tensor(out, in0, in1, op), reads=[in0, in1], writes=[out])

        def ts(eng, out, in0, s1, s2, op0, op1=None):
            rd = [in0] + [s for s in (s1, s2) if s is not None and not isinstance(s, (int, float))]
            if op1 is None:
                S.op(eng, lambda e: e.tensor_scalar(out, in0, s1, s2, op0), reads=rd, writes=[out])
            else:
                S.op(eng, lambda e: e.tensor_scalar(out, in0, s1, s2, op0, op1), reads=rd, writes=[out])

        def stt(eng, out, in0, scalar, in1, op0, op1):
            rd = [in0, in1] + ([] if isinstance(scalar, (int, float)) else [scalar])
            S.op(eng, lambda e: e.scalar_tensor_tensor(out, in0, scalar, in1, op0, op1), reads=rd, writes=[out])

        def recip_lp(out, in_):
            def fn(e):
                with nc.allow_low_precision(reason="fp32 reciprocal, result stored as a bf16 matmul operand"):
                    return e.reciprocal(out, in_)
            S.op("dve", fn, reads=[in_], writes=[out])

        def memset(eng, ap, val):
            S.op(eng, lambda e: e.memset(ap, val), writes=[ap])

        def dma_sp(out, in_, semkey, reads=(), writes=(), group=None):
            S.dma("sp", lambda e: e.dma_start(out=out, in_=in_), reads=reads, writes=writes, semkey=semkey, group=group)

        cst = B.alloc([128, 896], F32)
        ident32 = cst[:, 0:128]
        causal32 = cst[:, 128:256]
        ucum32 = cst[:, 256:384]
        maskadd = cst[:, 512:768]
        neghalf = cst[:, 768:769]
        dma_sp(cst, consts_d, "ld_cst", writes=[cst])
        cbf = B.alloc([128, 256], BF16)
        ident16 = cbf[:, 0:128]
        ones16 = cbf[:, 128:256]
        vcopy("dve", ident16, ident32)
        vcopy("dve", ones16, cst[:, 384:512])
        gnw = B.alloc([128, 512], F32)
        dma_sp(gnw, norms_d[:, 3 * D:3 * D + 512], "ld_gnw", writes=[gnw])
        bgate = B.alloc([128, 32], F32)
        dma_sp(bgate, bgate_d, "ld_bg", writes=[bgate])
        small = B.alloc([128, 64], F32)
        small_i = [0]

        def stat_col():
            i = small_i[0] % 64
            small_i[0] += 1
            return small[:, i:i + 1]

        NSLOT = 3
        SLOT_BYTES = 16384
        PREFETCH = 2
        wslot_raw = [B.alloc([128, SLOT_BYTES // 2], BF16) for _ in range(NSLOT)]
        plan = plan_in if plan_in is not None else []
        issued = []
        cursor = [0]

        def _views(i, spec):
            si = i % NSLOT
            views = []
            o = 0
            for (wname, kc, c, w) in spec:
                views.append(wslot_raw[si][:, o:o + kc * w].rearrange("p (k n) -> p k n", n=w))
                o += kc * w
            assert o * 2 <= SLOT_BYTES
            return views

        def _issue(i):
            spec = plan[i]
            si = i % NSLOT
            views = _views(i, spec)
            grp = ("wf", i)
            for (wname, kc, c, w), view in zip(spec, views):
                wv = WD[wname].rearrange("(k p) n -> p k n", p=128)
                src = wv[:, :, c:c + w]
                S.dma("pool", (lambda e, dst=view, src=src: e.dma_start(out=dst, in_=src)),
                      writes=[view], semkey=("wslot", si), group=grp)
            issued.append(views)

        def take(spec):
            i = cursor[0]
            cursor[0] += 1
            rec.append(spec)
            if plan_in is None:
                return _views(i, spec)
            assert plan[i] == spec, (i, plan[i], spec)

            def ndesc(sp):
                return sum(kc * 8 for (_, kc, _, _) in sp)

            while len(issued) < min(len(plan), i + 1 + PREFETCH):
                j = len(issued)
                if j > i and sum(ndesc(plan[t]) for t in range(i, j + 1)) > 800:
                    break
                _issue(j)
            return issued[i]

        class BankPool:
            def __init__(self, ids):
                self.ids = list(ids)
                self.i = 0

            def one(self):
                b_ = self.ids[self.i % len(self.ids)]
                self.i += 1
                return b_

            def group(self, n):
                ng = len(self.ids) // n
                g_ = self.i % ng
                self.i += 1
                return self.ids[g_ * n]

        ALLB = BankPool(range(8))
        LOB = BankPool(range(4))
        HIB = BankPool(range(4, 8))

        def nbank():
            return ALLB.one()

        def ngroup(n):
            return ALLB.group(n)

        def g_gemm_ws(wview, kc, nm, rhs_fn, ntq, epilogue, m0=0, mrows=128, bp=None, split=False):
            bp = bp or ALLB
            for m in range(nm):
                if split:
                    passes = [[0, 1], [2, 3]]
                else:
                    passes = [list(range(ntq))]
                for tqs in passes:
                    base = bp.group(len(tqs) if len(tqs) in (2, 4) else 4)
                    pss = [None] * ntq
                    for i_, tq in enumerate(tqs):
                        pss[tq] = pbank(base + i_)[0:mrows, :]
                    for k in range(kc):
                        for tq in tqs:
                            mm(pss[tq], wview[:, k, m * 128:m * 128 + mrows], rhs_fn(k, tq), start=(k == 0), stop=(k == kc - 1))
                        yield
                    epilogue(m0 + m, pss)
                    yield

        def run(g):
            for _ in g:
                pass

        def gemm_ws(*a_, **kw):
            run(g_gemm_ws(*a_, **kw))

        def interleave(ga, gb, ra=1, rb=1):
            da = db = False
            while not (da and db):
                if not da:
                    for _ in range(ra):
                        try:
                            next(ga)
                        except StopIteration:
                            da = True
                            break
                if not db:
                    for _ in range(rb):
                        try:
                            next(gb)
                        except StopIteration:
                            db = True
                            break

        def limited(g, n):
            for _ in range(n):
                try:
                    next(g)
                except StopIteration:
                    return
                yield

        evr = [0]

        def ev_eng():
            evr[0] += 1
            return "act" if evr[0] % 2 else "dve"

        WD = {"w_in": w_in_d, "w_a": w_a_d, "w_b": w_b_d, "w_out": w_out_d, "w_f1": w_f1_d, "w_f2": w_f2_d}

        def spec_gate(p):
            return [("w_in", KC, C_GA + p * 256, 256)]

        def spec_lr():
            return [("w_in", KC, C_LR, 16)]

        def spec_qk(h):
            return [("w_in", KC, C_Q + h * 256, 256), ("w_in", KC, C_K + h * 256, 256)]

        def spec_vg(h, which):
            return [("w_in", KC, (C_V if which == 0 else C_G) + h * 512, 512)]

        def spec_dil(j, g):
            return [("w_in", KC, C_DIL + (3 * g + t_) * 1024 + j * 128, 128) for t_ in range(3)]

        def spec_ab(p):
            return [("w_a", KC, p * 256, 256), ("w_b", 8, p * 256, 256)]

        def spec_out(p):
            return [("w_out", KC, p * 512, 512)]

        def spec_f1(p):
            return [("w_f1", KC, p * 256, 256), ("w_f1", KC, DFF + p * 256, 256)]

        def spec_f2(m):
            return [("w_f2", FC, m * 128, 128)]

        hT_off = B.mark()
        hT = B.alloc([128, KC, T], BF16)
        phase_mark = B.mark()
        hT_end = phase_mark

        def rms_stat(rows, width):
            ss = stat_col()
            act(rms_junk[:, 0:width], rows, AF.Square, accum=ss)
            rs = stat_col()
            act(rs, ss, AF.Ln, bias=EPS, scale=1.0 / width)
            act(rs, rs, AF.Exp, scale=-0.5)
            return rs

        def rms_rows(rows, width, gmul, out):
            rs = rms_stat(rows, width)
            stt("dve", out, rows, rs, gmul, ALU.mult, ALU.mult)

        def xpose_rows(h_, dstT, tb, eng=None):
            for half in range(2):
                pv = pbank(nbank(), BF16)
                for c in range(8):
                    cc = half * 8 + c
                    tr(pv[:, c * 128:(c + 1) * 128], h_[:, cc * 128:(cc + 1) * 128], ident16)
                vcopy(eng or ev_eng(), dstT[:, half * 8:half * 8 + 8, tb * 128:(tb + 1) * 128],
                      pv.rearrange("p (c t) -> p c t", t=128))

        def row_pipeline(ntb, Lf, Tf, Rf, Xf):
            if Lf:
                Lf(0)
            for i in range(ntb + 2):
                if Lf and i + 1 < ntb:
                    Lf(i + 1)
                if Tf and i < ntb:
                    Tf(i)
                if Rf and 0 <= i - 1 < ntb:
                    Rf(i - 1)
                if Xf and 0 <= i - 2 < ntb:
                    Xf(i - 2)

        rms_junk = B.alloc([128, D], BF16)
        gbc = B.alloc([128, D], F32)
        dma_sp(gbc, norms_d[:, 0:D], "ld_gbc", writes=[gbc])
        NX = 6
        NH0 = 3
        xt = [B.alloc([128, D], F32) for _ in range(NX)]
        hn0 = [B.alloc([128, D], BF16) for _ in range(NH0)]

        def x_load(tb):
            t_ = xt[tb % NX]
            dma_sp(t_, x_d[tb * 128:(tb + 1) * 128, :], ("ld_x", tb % NX), writes=[t_])

        for tb0 in range(NX - 2):
            x_load(tb0)
        row_pipeline(T // 128, None, (lambda tb: x_load(tb + NX - 2) if tb + NX - 2 < T // 128 else None),
                     lambda tb: rms_rows(xt[tb % NX], D, gbc, hn0[tb % NH0]),
                     lambda tb: xpose_rows(hn0[tb % NH0], hT, tb))
        B.reset(phase_mark)

        def hT_rhs(k, tq):
            return hT[:, k, tq * 512:(tq + 1) * 512]

        G = None
        if stages >= 1:
            nbg = B.alloc([128, 32], F32)
            ts("dve", nbg, bgate, -1.0, None, ALU.mult)
            gt = [B.alloc([128, 1024], BF16) for _ in range(2)]
            gtmp = [B.alloc([128, 512], F32) for _ in range(2)]
            gi = [0]

            def gate_epi(m, pss):
                for hf in range(2):
                    if pss[hf * 2] is None:
                        continue
                    t_ = gt[gi[0] % 2]
                    for q2 in range(2):
                        tq = hf * 2 + q2
                        tm = gtmp[q2]
                        act(t_[:, q2 * 512:(q2 + 1) * 512], pss[tq], AF.Exp, bias=nbg[:, m:m + 1], scale=-1.0)
                    dma_sp(gates_d[m][:, hf * 1024:(hf + 1) * 1024], t_, ("st_gt", gi[0] % 2), reads=[t_],
                           writes=[("gates", m, hf)])
                    gi[0] += 1

            def g_gates():
                for p in range(16):
                    (wv_,) = take(spec_gate(p))
                    yield from g_gemm_ws(wv_, KC, 2, hT_rhs, 4, gate_epi, m0=p * 2, bp=LOB, split=True)

            G = g_gates()
            if stages < 2:
                run(G)
            phase_mark = B.mark()

        if stages >= 2:
            scale_k = 256 ** -0.5
            lrT = B.alloc([17, T], F32)
            memset("dve", lrT, 1.0)
            (wv,) = take(spec_lr())

            def lr_epi(m, pss):
                for tq in range(4):
                    vcopy(ev_eng(), lrT[0:16, tq * 512:(tq + 1) * 512], pss[tq])

            gemm_ws(wv, KC, 1, hT_rhs, 4, lr_epi, mrows=16)
            gla_mark = B.mark()

            for h in range(4):
                B.reset(gla_mark)
                wgk = B.alloc([17, 256], F32)
                dma_sp(wgk, wgk_d[:, h * 256:(h + 1) * 256], "ld_wgk", writes=[wgk])
                qeT = B.alloc([128, 2, T], BF16)
                keT = B.alloc([128, 2, T], BF16)
                v_h = B.alloc([128, 16, 512], BF16)
                bT_off = B.mark()
                bT = B.alloc([128, 2, T], F32)
                dec = B.alloc([128, 2, 16], F32)
                st32 = B.alloc([128, 2, 512], F32)
                st16 = B.alloc([128, 2, 512], BF16)
                tmp_mark = B.mark()
                B.reset(bT_off)
                sg_h = B.alloc([128, 16, 512], BF16)
                B.reset(tmp_mark)
                gkr = [B.alloc([128, 256], F32) for _ in range(2)]
                e1 = [B.alloc([128, 256], F32) for _ in range(2)]
                B.reset(tmp_mark)
                tmp512 = [B.alloc([128, 512], F32) for _ in range(2)]
                attb = [B.alloc([128, 128], BF16) for _ in range(2)]
                onb = [B.alloc([128, 512], BF16) for _ in range(2)]
                sdt = [B.alloc([128, 512], F32) for _ in range(2)]
                ket = [B.alloc([128, 256], BF16) for _ in range(2)]
                ogc = [B.alloc([128, 4, 128], BF16) for _ in range(2)]
                rms_junk = B.alloc([128, 512], BF16)

                def g_decay():
                    pp = BankPool([6, 7])
                    bkA, bkB = 4, 5

                    def stA(tb):
                        pu = pbank(pp.one(), F32, 256)
                        mm(pu, lrT[0:17, tb * 128:(tb + 1) * 128], wgk[0:17, :])
                        a1 = e1[tb % 2]
                        act(a1, pu, AF.Exp, scale=-1.0)
                        act(gkr[tb % 2], a1, AF.Ln, bias=1.0)

                    def stB(tb):
                        t4 = tb % 4
                        g1 = gkr[tb % 2]
                        for kc, bk in ((0, bkA), (1, bkB)):
                            mm(pbank(bk, F32, 128, t4 * 128), g1[:, kc * 128:(kc + 1) * 128], ucum32)
                        if t4 == 3:
                            tbg = tb // 4
                            vcopy("dve", bT[:, 0, tbg * 512:(tbg + 1) * 512], pbank(bkA))
                            vcopy("dve", bT[:, 1, tbg * 512:(tbg + 1) * 512], pbank(bkB))

                    stA(0)
                    yield
                    for tb in range(16):
                        if tb + 1 < 16:
                            stA(tb + 1)
                            yield
                        stB(tb)
                        yield

                if G is not None:
                    interleave(g_decay(), limited(G, 68), 1, 2)
                else:
                    run(g_decay())
                act(dec, bT.rearrange("p k (c t) -> p k c t", t=128)[:, :, :, 127], AF.Exp)

                wq, wk = take(spec_qk(h))
                ti = [0]

                def mk_qk_epi(isq):
                    def epi(m, pss):
                        kc = m
                        for tq in range(4):
                            tm = tmp512[ti[0] % 2]
                            ti[0] += 1
                            sl = slice(tq * 512, (tq + 1) * 512)
                            if isq:
                                act(tm, bT[:, kc, sl], AF.Exp)
                                stt("dve", qeT[:, kc, sl], pss[tq], scale_k, tm, ALU.mult, ALU.mult)
                            else:
                                act(tm, bT[:, kc, sl], AF.Exp, scale=-1.0)
                                tt("dve", keT[:, kc, sl], pss[tq], tm, ALU.mult)
                    return epi

                gemm_ws(wq, KC, 2, hT_rhs, 4, mk_qk_epi(True))
                gemm_ws(wk, KC, 2, hT_rhs, 4, mk_qk_epi(False))

                for which in range(2):
                    (wv,) = take(spec_vg(h, which))
                    for tbg in range(4):
                        base = ngroup(4)
                        for k in range(KC):
                            for t4 in range(4):
                                tb = tbg * 4 + t4
                                mm(pbank(base + t4), hT[:, k, tb * 128:(tb + 1) * 128], wv[:, k, :], start=(k == 0), stop=(k == KC - 1))
                        for t4 in range(4):
                            tb = tbg * 4 + t4
                            if which == 0:
                                vcopy(ev_eng(), v_h[:, tb, :], pbank(base + t4))
                            else:
                                tm = tmp512[ti[0] % 2]
                                ti[0] += 1
                                act(tm, pbank(base + t4), AF.Silu)
                                tt("dve", sg_h[:, tb, :], tm, gnw, ALU.mult)

                memset("dve", st32, 0.0)
                memset("dve", st16, 0.0)

                def g_rec(h=h, qeT=qeT, keT=keT, v_h=v_h, sg_h=sg_h, st32=st32, st16=st16, dec=dec, attb=attb, onb=onb,
                          sdt=sdt, ket=ket, ogc=ogc):
                    bS, bO, bK = 4, 5, (6, 7)

                    def att_mask(c):
                        csl = slice(c * 128, (c + 1) * 128)
                        pa = pbank(bS, F32, 128, 0)
                        for kc in range(2):
                            mm(pa, keT[:, kc, csl], qeT[:, kc, csl], start=(kc == 0), stop=(kc == 1))
                        tt("dve", attb[c % 2], pa, causal32, ALU.mult)

                    def k_T(c):
                        csl = slice(c * 128, (c + 1) * 128)
                        pvk = pbank(bS, BF16, 256, 256)
                        for kc in range(2):
                            tr(pvk[:, kc * 128:(kc + 1) * 128], keT[:, kc, csl], ident16)
                        vcopy("act", ket[c % 2], pvk)

                    def out_T(c):
                        csl = slice(c * 128, (c + 1) * 128)
                        ob = onb[c % 2]
                        pv = pbank(bS, BF16, 512, 512)
                        for j in range(4):
                            tr(pv[:, j * 128:(j + 1) * 128], ob[:, j * 128:(j + 1) * 128], ident16)
                        oc = ogc[c % 2]
                        vcopy("act", oc, pv.rearrange("p (j t) -> p j t", t=128))
                        dma_sp(oglaT_d[h * 4:(h + 1) * 4, :, csl].rearrange("c p t -> p c t"), oc, ("st_og", c % 2),
                               reads=[oc], writes=[("oglaT", h, c)])

                    att_mask(0)
                    k_T(0)
                    yield
                    for c in range(16):
                        csl = slice(c * 128, (c + 1) * 128)
                        po = pbank(bO)
                        mm(po, attb[c % 2], v_h[:, c, :], start=True, stop=False)
                        for kc in range(2):
                            mm(po, qeT[:, kc, csl], st16[:, kc, :], start=False, stop=(kc == 1))
                        rs = rms_stat(po, 512)
                        yield
                        if c < 15:
                            kt = ket[c % 2]
                            pks = []
                            for kc in range(2):
                                pk = pbank(bK[kc])
                                mm(pk, kt[:, kc * 128:(kc + 1) * 128], v_h[:, c, :])
                                pks.append(pk)
                            att_mask(c + 1)
                            for kc in range(2):
                                sd = sdt[kc]
                                act(sd, st32[:, kc, :], AF.Copy, scale=dec[:, kc, c:c + 1])
                                stt("dve", st16[:, kc, :], pks[kc], dec[:, kc, c:c + 1], sd, ALU.mult, ALU.add)
                            for kc in range(2):
                                stt("dve", st32[:, kc, :], pks[kc], dec[:, kc, c:c + 1], sdt[kc], ALU.mult, ALU.add)
                            yield
                        stt("dve", onb[c % 2], po, rs, sg_h[:, c, :], ALU.mult, ALU.mult)
                        if c + 1 < 15:
                            k_T(c + 1)
                        yield
                        if c >= 1:
                            out_T(c - 1)
                            yield
                    out_T(15)
                    yield

                if G is not None:
                    interleave(g_rec(), limited(G, 204), 1, 3)
                else:
                    run(g_rec())
            if G is not None:
                run(G)
            phase_mark = hT_end
            B.reset(phase_mark)

        if stages >= 3:
            scale_a = 128 ** -0.5
            B.reset(phase_mark)
            XT = [[B.alloc([128, T], BF16) for _ in range(3)] for _ in range(3)]
            vtok = [B.alloc([128, 16, 128], BF16) for _ in range(3)]
            acc = B.alloc([128, 2, T], F32)
            bm = B.alloc([128, 3, 256], F32)
            NR = 4
            stt_ = [B.alloc([128, 256], F32) for _ in range(NR)]
            ptl = [B.alloc([128, 256], BF16) for _ in range(NR)]
            odT = B.alloc([128, T], BF16)
            items = [(j, g) for j in range(8) for g in range(3)]

            def g_P(j, g):
                d = DILS[g]
                dma_sp(bm[:, g, :], btab_d[:, g * 8 + j, :], ("ld_bm", g), writes=[bm[:, g, :]])
                tt("dve", bm[:, g, :], bm[:, g, :], maskadd, ALU.add)
                wvs = take(spec_dil(j, g))
                for t_ in range(3):
                    def perm_epi(m, pss, g=g, d=d, t_=t_):
                        dst = XT[g][t_]
                        for tq in range(4):
                            if pss[tq] is None:
                                continue
                            if d == 1:
                                vcopy(ev_eng(), dst[:, tq * 512:(tq + 1) * 512], pss[tq])
                            else:
                                n = 512 // d
                                dv = dst.rearrange("p (r m) -> p r m", r=d)[:, :, tq * n:(tq + 1) * n]
                                vcopy(ev_eng(), dv, pss[tq].rearrange("p (m r) -> p r m", r=d))
                    yield from g_gemm_ws(wvs[t_], KC, 1, hT_rhs, 4, perm_epi, bp=LOB, split=True)
                for half in range(2):
                    pv = pbank(LOB.one(), BF16)
                    for b8 in range(8):
                        blk = half * 8 + b8
                        tr(pv[:, b8 * 128:(b8 + 1) * 128], XT[g][2][:, blk * 128:(blk + 1) * 128], ident16)
                    vcopy(ev_eng(), vtok[g][:, half * 8:(half + 1) * 8, :], pv.rearrange("p (b e) -> p b e", e=128))
                    yield

            uctr = [0]

            def g_A(j, g):
                d = DILS[g]
                L = T // d
                nblk = L // 128
                units = [(r, n) for r in range(d) for n in range(nblk)]

                def qk(u):
                    r, n = units[u]
                    blk = r * nblk + n
                    qb = XT[g][0][:, blk * 128:(blk + 1) * 128]
                    bks = HIB.one()
                    ps_s = pbank(bks, F32, 256)
                    parts = ([0] if n > 0 else []) + [1]
                    for x_ in parts:
                        kb = blk - 1 + x_
                        mm(ps_s[:, x_ * 128:(x_ + 1) * 128], XT[g][1][:, kb * 128:(kb + 1) * 128], qb)
                    lo = parts[0] * 128
                    sb = stt_[uctr[0] % NR]
                    pt = ptl[uctr[0] % NR]
                    uctr[0] += 1
                    stt("dve", sb[:, lo:256], ps_s[:, lo:256], scale_a, bm[:, g, lo:256], ALU.mult, ALU.add)
                    act(pt[:, lo:256], sb[:, lo:256], AF.Exp)
                    return (bks, parts, blk, pt, r, n)

                def pv_(ctx):
                    bks, parts, blk, pt, r, n = ctx
                    ps_n = pbank(bks, F32, 256, 256)
                    for ii, x_ in enumerate(parts):
                        kb = blk - 1 + x_
                        mm(ps_n[:, 0:128], vtok[g][:, kb, :], pt[:, x_ * 128:(x_ + 1) * 128],
                           start=(ii == 0), stop=(ii == len(parts) - 1))
                    for ii, x_ in enumerate(parts):
                        mm(ps_n[:, 128:256], ones16, pt[:, x_ * 128:(x_ + 1) * 128],
                           start=(ii == 0), stop=(ii == len(parts) - 1))
                    av = acc.rearrange("p x (m r) -> p x m r", r=d)[:, :, n * 128:(n + 1) * 128, r]
                    pn = ps_n.rearrange("p (x q) -> p x q", q=128)
                    if g == 0:
                        vcopy("act", av, pn)
                    else:
                        tt("dve", av, pn, av, ALU.add)

                LA = 2
                nu = len(units)
                ctxs = {}
                for u0 in range(min(LA, nu)):
                    ctxs[u0] = qk(u0)
                    yield
                for u in range(nu):
                    if u + LA < nu:
                        ctxs[u + LA] = qk(u + LA)
                        yield
                    pv_(ctxs.pop(u))
                    yield
                if g == 2:
                    act(acc[:, 1, :], acc[:, 1, :], AF.Ln)
                    act(acc[:, 1, :], acc[:, 1, :], AF.Exp, scale=-1.0)
                    tt("dve", odT, acc[:, 0, :], acc[:, 1, :], ALU.mult)
                    dma_sp(odilT_d[j], odT, "st_odT", reads=[odT], writes=[("odilT", j)])
                    yield

            run(g_P(*items[0]))
            for i, it in enumerate(items):
                if i + 1 < len(items):
                    interleave(g_A(*it), g_P(*items[i + 1]), 1, 3)
                else:
                    run(g_A(*it))
            B.reset(phase_mark)

        TH = T // 2
        post_mark = hT_off

        NRS = 3

        def rows_load(tb, res_d, r0, ybl, xrl):
            xr = xrl[tb % NRS]
            if ybl is not None:
                yb = ybl[tb % NRS]
                dma_sp(yb, y_d[:, :, tb * 128:(tb + 1) * 128].rearrange("c p t -> p c t"), ("ld_yb", tb % NRS),
                       reads=[("y", m) for m in range(16)], writes=[yb])
            dma_sp(xr, res_d[r0 + tb * 128:r0 + (tb + 1) * 128, :], ("ld_xr", tb % NRS),
                   reads=[("x1", r0 // TH, tb)] if res_d is x1_d else [], writes=[xr])

        def rows_add(tb, ybl, xrl, ysb=None):
            xr = xrl[tb % NRS]
            for cg in range(4):
                pv = pbank(nbank())
                for i in range(4):
                    if callable(ysb):
                        src = ysb(cg * 4 + i, tb)
                    elif ysb is not None:
                        src = ysb[:, cg * 4 + i, tb * 128:(tb + 1) * 128]
                    else:
                        src = ybl[tb % NRS][:, cg * 4 + i, :]
                    tr(pv[:, i * 128:(i + 1) * 128], src, ident32)
                tt("dve", xr[:, cg * 512:(cg + 1) * 512], pv, xr[:, cg * 512:(cg + 1) * 512], ALU.add)
            return xr

        if stages >= 4:
            for h2 in range(2):
                B.reset(post_mark)
                tsl = slice(h2 * TH, (h2 + 1) * TH)
                mgT = B.alloc([128, KC, TH], BF16)
                s1_mark = B.mark()
                ogh = B.alloc([128, 16, TH], BF16)
                odh = B.alloc([128, 8, TH], BF16)
                gab = [B.alloc([128, 2, TH], BF16) for _ in range(2)]
                gfl = [B.alloc([128, 2, TH], F32) for _ in range(2)]
                t1 = [B.alloc([128, 512], F32) for _ in range(2)]
                t2 = [B.alloc([128, 512], F32) for _ in range(2)]
                dma_sp(ogh, oglaT_d[:, :, tsl].rearrange("c p t -> p c t"), "ld_ogh",
                       reads=[("oglaT", i, c) for i in range(4) for c in range(16)], writes=[ogh])
                dma_sp(odh, odilT_d[:, :, tsl].rearrange("c p t -> p c t"), "ld_odh",
                       reads=[("odilT", i) for i in range(8)], writes=[odh])
                for p in range(8):
                    wa, wb = take(spec_ab(p))
                    for mi in range(2):
                        m = p * 2 + mi
                        gb_ = gab[m % 2]
                        dma_sp(gb_[:, 0, :], gates_d[m][:, tsl], ("ld_ga", m % 2), reads=[("gates", m, h2)], writes=[gb_[:, 0, :]])
                        dma_sp(gb_[:, 1, :], gates_d[16 + m][:, tsl], ("ld_gb", m % 2), reads=[("gates", 16 + m, h2)], writes=[gb_[:, 1, :]])
                        gf_ = gfl[m % 2]
                        act(gf_, gb_, AF.Ln, bias=1.0)
                        act(gf_, gf_, AF.Exp, scale=-1.0)
                        base = ngroup(4)
                        pA = [pbank(base + tq) for tq in range(2)]
                        pB = [pbank(base + 2 + tq) for tq in range(2)]
                        for k in range(KC):
                            for tq in range(2):
                                mm(pA[tq], wa[:, k, mi * 128:(mi + 1) * 128], ogh[:, k, tq * 512:(tq + 1) * 512],
                                   start=(k == 0), stop=(k == KC - 1))
                        for k in range(8):
                            for tq in range(2):
                                mm(pB[tq], wb[:, k, mi * 128:(mi + 1) * 128], odh[:, k, tq * 512:(tq + 1) * 512],
                                   start=(k == 0), stop=(k == 7))
                        for tq in range(2):
                            sl = slice(tq * 512, (tq + 1) * 512)
                            tt("dve", t1[tq], pA[tq], gf_[:, 0, sl], ALU.mult)
                            tt("dve", t2[tq], pB[tq], gf_[:, 1, sl], ALU.mult)
                            tt("dve", mgT[:, m, sl], t1[tq], t2[tq], ALU.add)
                B.reset(s1_mark)
                yall = B.alloc([128, 16, TH], F32)

                def y_sb_epi(m, pss):
                    for tq in range(2):
                        vcopy(ev_eng(), yall[:, m, tq * 512:(tq + 1) * 512], pss[tq])

                yT = None
                yi = [0]

                def y_epi(m, pss):
                    y = yT[yi[0] % 2]
                    for tq in range(2):
                        vcopy(ev_eng(), y[:, tq * 512:(tq + 1) * 512], pss[tq])
                    dma_sp(y_d[m], y, ("st_y", yi[0] % 2), reads=[y], writes=[("y", m)])
                    yi[0] += 1

                for p in range(4):
                    (wo,) = take(spec_out(p))
                    gemm_ws(wo, KC, 4, lambda k, tq: mgT[:, k, tq * 512:(tq + 1) * 512], 2, y_sb_epi, m0=p * 4)
                s2b_mark = B.mark()
                B.reset(post_mark)
                hfT = B.alloc([128, KC, TH], BF16)
                B.reset(s2b_mark)
                xrl = [B.alloc([128, D], F32) for _ in range(NRS)]
                gbc2 = B.alloc([128, D], F32)
                hn2 = [B.alloc([128, D], BF16) for _ in range(2)]
                rms_junk = B.alloc([128, D], BF16)
                dma_sp(gbc2, norms_d[:, D:2 * D], "ld_gbc2", writes=[gbc2])

                def s2b_T(tb):
                    xr = rows_add(tb, None, xrl, ysb=yall)
                    dma_sp(x1_d[h2 * TH + tb * 128:h2 * TH + (tb + 1) * 128, :], xr, ("st_x1", tb % NRS), reads=[xr],
                           writes=[("x1", h2, tb)])

                row_pipeline(8, lambda tb: rows_load(tb, x_d, h2 * TH, None, xrl), s2b_T,
                             (lambda tb: rms_rows(xrl[tb % NRS], D, gbc2, hn2[tb % 2])) if stages >= 5 else None,
                             (lambda tb: xpose_rows(hn2[tb % 2], hfT, tb, eng="act")) if stages >= 5 else None)
                if stages >= 5:
                    B.reset(s1_mark)
                    actT = B.alloc([128, FC, TH], BF16)
                    act_end = B.mark()
                    ft = [B.alloc([128, 512], F32) for _ in range(2)]
                    fi = [0]
                    for p in range(22):
                        wg_, wu_ = take(spec_f1(p))
                        for fi_ in range(2):
                            f = p * 2 + fi_
                            base = ngroup(4)
                            pG = [pbank(base + tq) for tq in range(2)]
                            pU = [pbank(base + 2 + tq) for tq in range(2)]
                            for k in range(KC):
                                for tq in range(2):
                                    mm(pG[tq], wg_[:, k, fi_ * 128:(fi_ + 1) * 128], hfT[:, k, tq * 512:(tq + 1) * 512],
                                       start=(k == 0), stop=(k == KC - 1))
                            for k in range(KC):
                                for tq in range(2):
                                    mm(pU[tq], wu_[:, k, fi_ * 128:(fi_ + 1) * 128], hfT[:, k, tq * 512:(tq + 1) * 512],
                                       start=(k == 0), stop=(k == KC - 1))
                            for tq in range(2):
                                tm = ft[fi[0] % 2]
                                fi[0] += 1
                                act(tm, pG[tq], AF.Silu)
                                tt("dve", actT[:, f, tq * 512:(tq + 1) * 512], tm, pU[tq], ALU.mult)
                    B.reset(post_mark)
                    yA = B.alloc([128, 8, TH], F32)
                    B.reset(act_end)
                    yB = B.alloc([128, 8, TH], F32)

                    def y2_epi(m, pss):
                        dst = yA if m < 8 else yB
                        for tq in range(2):
                            vcopy(ev_eng(), dst[:, m % 8, tq * 512:(tq + 1) * 512], pss[tq])

                    for m in range(16):
                        (wv,) = take(spec_f2(m))
                        gemm_ws(wv, FC, 1, lambda k, tq: actT[:, k, tq * 512:(tq + 1) * 512], 2, y2_epi, m0=m)
                    B.reset(s1_mark)
                    xrl = [B.alloc([128, D], F32) for _ in range(NRS)]
                    gbc3 = B.alloc([128, D], F32)
                    rms_junk = B.alloc([128, D], BF16)
                    orow = [B.alloc([128, D], F32) for _ in range(2)]
                    assert B.mark() <= act_end
                    dma_sp(gbc3, norms_d[:, 2 * D:3 * D], "ld_gbc3", writes=[gbc3])

                    def y2_src(c, tb):
                        return (yA if c < 8 else yB)[:, c % 8, tb * 128:(tb + 1) * 128]

                    def s4b_R(tb):
                        o_ = orow[tb % 2]
                        rms_rows(xrl[tb % NRS], D, gbc3, o_)
                        r0 = h2 * TH + tb * 128
                        dma_sp(out_d[r0:r0 + 128, :], o_, ("st_out", tb % 2), reads=[o_], writes=[("out", h2, tb)])

                    row_pipeline(8, lambda tb: rows_load(tb, x1_d, h2 * TH, None, xrl),
                                 lambda tb: rows_add(tb, None, xrl, ysb=y2_src), s4b_R, None)

        final_reads = []
        for k in list(S.last_writer.keys()):
            if isinstance(k, tuple) and k[0] in ("out", "x1", "oglaT", "odilT", "gates", "y"):
                final_reads.append(k)
        S.op("sp", None, reads=final_reads)
        assert plan_in is None or cursor[0] == len(plan), (cursor[0], len(plan))
        S.emit(st)
        print(f"[build] ops={len(S.ops)} sems={S.nsem}")
    return nc


def _t5_bucket(dist):
    max_exact = 16
    d = np.maximum(dist, 1).astype(np.float64)
    large = max_exact + (np.log(d / max_exact) / math.log(2048 / max_exact) * (32 - max_exact)).astype(np.int64)
    large = np.minimum(large, 31)
    return np.where(dist < max_exact, dist, large).astype(np.int64)


def _host_consts():
    c = np.zeros((128, 896), np.float32)
    j = np.arange(128)[:, None]
    i = np.arange(128)[None, :]
    c[:, 0:128] = np.eye(128, dtype=np.float32)
    c[:, 128:256] = (j <= i)
    c[:, 256:384] = np.where(j <= i, -1.0 / 16.0, 0.0)
    c[:, 384:512] = 1.0
    c[:, 512:640] = np.where(j >= i, 0.0, NEG)
    c[:, 640:768] = np.where(j <= i, 0.0, NEG)
    c[:, 768:896] = -0.5
    return c


def _bias_index():
    cidx = np.arange(128)[:, None]
    a = np.arange(128)[None, :]
    idx = np.zeros((3, 128, 2, 128), np.int64)
    for g, d in enumerate(DILS):
        steps_prev = 128 + a - cidx
        steps_cur = a - cidx
        idx[g, :, 0, :] = _t5_bucket(np.clip(steps_prev, 0, None) * d)
        idx[g, :, 1, :] = _t5_bucket(np.clip(steps_cur, 0, None) * d)
    return idx


_NC_CACHE = {}


def make_in_maps(inputs):
    f = lambda a: np.ascontiguousarray(np.asarray(a, dtype=np.float32))
    x = f(inputs["x"])
    norms = np.concatenate([f(inputs["attn_norm"])[0], f(inputs["ffn_norm"])[0], f(inputs["final_norm"]),
                            f(inputs["gla_norm"])[0]])
    norms = np.ascontiguousarray(np.broadcast_to(norms[None, :], (128, norms.shape[0])))
    wgk = np.ascontiguousarray(np.concatenate([f(inputs["w_gk_up"])[0], f(inputs["b_gk"])[0][None, :]], axis=0))
    bgate = np.ascontiguousarray(f(inputs["b_gate"])[0].reshape(32, 128).T)
    rb = f(inputs["rel_bias"])
    idx = _bias_index()
    btab = np.zeros((128, 24, 256), np.float32)
    for g in range(3):
        for s in range(8):
            hh = g * 8 + s
            btab[:, hh, :] = rb[:, hh][idx[g]].reshape(128, 256)
    shared = {
        "w_in": f(inputs["w_in"])[0], "w_a": f(inputs["w_branch_gla"])[0], "w_b": f(inputs["w_branch_dil"])[0],
        "w_out": f(inputs["w_out"])[0], "w_f1": f(inputs["w_ffn_in"])[0], "w_f2": f(inputs["w_ffn_out"])[0],
        "norms": norms, "consts": _host_consts(), "wgk": wgk, "bgate": bgate, "btab": btab,
    }
    return [dict(shared, x=np.ascontiguousarray(x[b])) for b in range(x.shape[0])]


def kernel(**inputs):
    in_maps = make_in_maps(inputs)
    if "nc" not in _NC_CACHE:
        _NC_CACHE["nc"] = build()
    nc = _NC_CACHE["nc"]
    res = run_bass_kernel_spmd(nc, in_maps, core_ids=list(range(len(in_maps))))
    out = np.stack([np.asarray(r["out"], dtype=np.float32) for r in res.results], axis=0)
    return out
```

```python
import math
from contextlib import ExitStack

import numpy as np
import concourse.bass as bass
import concourse.mybir as mybir
from concourse.bass_utils import run_bass_kernel_spmd

F32 = mybir.dt.float32
BF16 = mybir.dt.bfloat16
AF = mybir.ActivationFunctionType
ALU = mybir.AluOpType

T = 2048
D = 2048
KC = D // 128
DFF = 5632
FC = DFF // 128
DIN = 19472
C_Q, C_K, C_V, C_G, C_LR, C_DIL, C_GA = 0, 1024, 2048, 4096, 6144, 6160, 15376
EPS = 1e-6
NEG = -30000.0
DILS = (1, 4, 16)

ARENA_BYTES = 207 * 1024 + 512
PAGE = 512


def _dtsize(dt):
    return 4 if dt in (F32, mybir.dt.float32r, mybir.dt.int32, mybir.dt.uint32) else 2


class _Op:
    __slots__ = ("eng", "fn", "dma", "deps", "signal", "sigsem", "sigval", "idx", "semkey", "group")


class Sched:
    ENGS = ("pe", "act", "dve", "pool", "sp")
    SEM_ROLL = 30000

    def __init__(self, nc):
        self.nc = nc
        self.ops = []
        self.last_writer = {}
        self.readers = {}

    @staticmethod
    def keys_of(x):
        if isinstance(x, (tuple, str)):
            return [x]
        esz = _dtsize(x.dtype)
        lo = int(x.offset) * esz
        span = esz
        for (step, cnt) in x.ap[1:]:
            span += (cnt - 1) * abs(step) * esz
        hi = lo + span - 1
        nm = x.name
        if nm == "psum":
            return [("psum", p) for p in range(lo // 2048, hi // 2048 + 1)]
        return [(nm, p) for p in range(lo // PAGE, hi // PAGE + 1)]

    def op(self, eng, fn, reads=(), writes=(), dma=False, semkey=None, group=None):
        o = _Op()
        o.eng, o.fn, o.dma, o.semkey, o.group = eng, fn, dma, semkey, group
        o.signal = False
        o.sigsem = None
        o.sigval = None
        o.idx = len(self.ops)
        rk = []
        wk = []
        for r in reads:
            for k_ in self.keys_of(r):
                (wk if k_[0] == "psum" else rk).append(k_)
        for w in writes:
            wk.extend(self.keys_of(w))
        deps = {}
        lw = self.last_writer
        rd = self.readers
        for r in rk:
            w = lw.get(r)
            if w is not None:
                deps[w.idx] = w
        for k in wk:
            w = lw.get(k)
            if w is not None:
                deps[w.idx] = w
            for r_ in rd.get(k, ()):
                deps[r_.idx] = r_
        o.deps = list(deps.values())
        for r in rk:
            l = rd.get(r)
            if l is None:
                rd[r] = [o]
            elif not l or l[-1] is not o:
                l.append(o)
        for k in wk:
            lw[k] = o
            rd[k] = []
        self.ops.append(o)
        return o

    def dma(self, eng, fn, reads=(), writes=(), semkey=None, group=None):
        assert semkey is not None
        return self.op(eng, fn, reads, writes, dma=True, semkey=semkey, group=group)

    def emit(self, stack):
        nc = self.nc
        for o in self.ops:
            for d in o.deps:
                if d.dma:
                    continue
                if d.eng == o.eng and o.eng == "pe" and not o.dma:
                    continue
                d.signal = True
        cur_sem, cur_cnt = {}, {}
        nsem = [0]

        def new_sem(tag):
            nsem[0] += 1
            return stack.enter_context(nc.semaphore(f"s{tag}{nsem[0]}"))

        dsem, dcnt = {}, {}
        groups = {}
        for o in self.ops:
            if o.dma:
                k = o.semkey
                if k not in dsem or (dcnt[k] > self.SEM_ROLL and (o.group is None or o.group not in groups)):
                    dsem[k] = new_sem("d")
                    dcnt[k] = 0
                dcnt[k] += 16
                o.sigsem = dsem[k]
                o.sigval = dcnt[k]
                if o.group is not None:
                    groups.setdefault(o.group, []).append(o)
            elif o.signal:
                e = o.eng
                if e not in cur_sem or cur_cnt[e] >= self.SEM_ROLL:
                    cur_sem[e] = new_sem(e)
                    cur_cnt[e] = 0
                cur_cnt[e] += 1
                o.sigsem = cur_sem[e]
                o.sigval = cur_cnt[e]
        for g, members in groups.items():
            mx = max(m.sigval for m in members)
            for m in members:
                assert m.sigsem is members[0].sigsem
                m.sigval = mx
        self.nsem = nsem[0]
        by_eng = {e: [] for e in self.ENGS}
        for o in self.ops:
            by_eng[o.eng].append(o)

        def run_queue(eh, ops):
            waited = {}
            for o in ops:
                need = {}
                for d in o.deps:
                    if (not d.dma) and d.eng == o.eng and o.eng == "pe" and not o.dma:
                        continue
                    k = id(d.sigsem)
                    if k not in need or need[k][1] < d.sigval:
                        need[k] = (d.sigsem, d.sigval)
                for k, (sem, val) in need.items():
                    if waited.get(k, 0) >= val:
                        continue
                    eh.wait_ge(sem, val)
                    waited[k] = val
                if o.fn is None:
                    continue
                ins = o.fn(eh)
                if o.dma:
                    ins.then_inc(o.sigsem, 16)
                elif o.signal:
                    ins.then_inc(o.sigsem, 1)

        block = stack.enter_context(nc.Block())

        @block.tensor
        def _(e):
            run_queue(e, by_eng["pe"])

        @block.scalar
        def _(e):
            run_queue(e, by_eng["act"])

        @block.vector
        def _(e):
            run_queue(e, by_eng["dve"])

        @block.gpsimd
        def _(e):
            run_queue(e, by_eng["pool"])

        @block.sync
        def _(e):
            run_queue(e, by_eng["sp"])


def build(stages=99, debug=False):
    rec = []
    _build(stages, debug, None, rec)
    return _build(stages, debug, rec, [])


def _build(stages, debug, plan_in, rec):
    nc = bass.Bass("TRN2", target_bir_lowering=False)
    okind = "ExternalOutput" if debug else "Internal"

    def din(name, shape):
        return nc.dram_tensor(name, list(shape), F32, kind="ExternalInput").ap()

    x_d = din("x", [T, D])
    w_in_d = din("w_in", [D, DIN])
    w_a_d = din("w_a", [D, D])
    w_b_d = din("w_b", [1024, D])
    w_out_d = din("w_out", [D, D])
    w_f1_d = din("w_f1", [D, 2 * DFF])
    w_f2_d = din("w_f2", [DFF, D])
    norms_d = din("norms", [128, 3 * D + 512])
    consts_d = din("consts", [128, 896])
    wgk_d = din("wgk", [17, 1024])
    bgate_d = din("bgate", [128, 32])
    btab_d = din("btab", [128, 24, 256])
    out_d = nc.dram_tensor("out", [T, D], F32, kind="ExternalOutput").ap()

    gates_d = nc.dram_tensor("gates_s", [32, 128, T], BF16, kind=okind).ap()
    oglaT_d = nc.dram_tensor("oglaT_s", [16, 128, T], BF16, kind=okind).ap()
    odilT_d = nc.dram_tensor("odilT_s", [8, 128, T], BF16, kind=okind).ap()
    x1_d = nc.dram_tensor("x1_s", [T, D], F32, kind=okind).ap()
    y_d = nc.dram_tensor("y_s", [16, 128, T // 2], F32).ap()

    st = ExitStack()
    with st:
        S = Sched(nc)
        arena = st.enter_context(nc.sbuf_tensor("arena", [128, ARENA_BYTES // 4], F32))
        psum = st.enter_context(nc.psum_tensor("psum", [128, 4096], F32))

        class Bump:
            def __init__(self, lo, hi):
                self.lo, self.hi, self.cur = lo, hi, lo

            def alloc(self, shape, dt):
                esz = _dtsize(dt)
                n = int(np.prod(shape[1:]))
                nb = (n * esz + PAGE - 1) // PAGE * PAGE
                off = self.cur
                assert off + nb <= self.hi, f"arena overflow {off + nb} > {self.hi}"
                self.cur += nb
                a = arena[0:shape[0], off // 4:(off + n * esz + 3) // 4]
                if dt != F32:
                    a = a.bitcast(dt)
                if len(shape) == 3:
                    a = a.rearrange("p (a b) -> p a b", b=shape[2])
                elif len(shape) == 4:
                    a = a.rearrange("p (a b c) -> p a b c", b=shape[2], c=shape[3])
                return a

            def mark(self):
                return self.cur

            def reset(self, m):
                self.cur = m

        B = Bump(0, ARENA_BYTES)

        def pbank(b, dt=F32, cols=None, off=0):
            a = psum[:, b * 512:(b + 1) * 512]
            if dt != F32:
                a = a.bitcast(dt)
            if cols is not None:
                a = a[:, off:off + cols]
            return a

        def mm(out, lhsT, rhs, start=True, stop=True):
            S.op("pe", lambda e: e.matmul(out, lhsT=lhsT, rhs=rhs, start=start, stop=stop),
                 reads=[lhsT, rhs], writes=[out])

        def tr(out, in_, ident):
            S.op("pe", lambda e: e.transpose(out, in_, ident), reads=[in_, ident], writes=[out])

        def act(out, in_, func, bias=None, scale=None, accum=None, extra_reads=()):
            kw = {}
            rd = [in_] + list(extra_reads)
            if bias is not None:
                kw["bias"] = bias
                if not isinstance(bias, (int, float)):
                    rd.append(bias)
            if scale is not None:
                kw["scale"] = scale
                if not isinstance(scale, (int, float)):
                    rd.append(scale)
            wr = [out]
            if accum is not None:
                kw["accum_out"] = accum
                wr.append(accum)
            S.op("act", lambda e: e.activation(out=out, in_=in_, func=func, **kw), reads=rd, writes=wr)

        def vcopy(eng, out, in_):
            if eng == "act":
                S.op("act", lambda e: e.copy(out, in_), reads=[in_], writes=[out])
            else:
                S.op(eng, lambda e: e.tensor_copy(out, in_), reads=[in_], writes=[out])

        def tt(eng, out, in0, in1, op):
            S.op(eng, lambda e: e.tensor_tensor(out, in0, in1, op), reads=[in0, in1], writes=[out])

        def ts(eng, out, in0, s1, s2, op0, op1=None):
            rd = [in0] + [s for s in (s1, s2) if s is not None and not isinstance(s, (int, float))]
            if op1 is None:
                S.op(eng, lambda e: e.tensor_scalar(out, in0, s1, s2, op0), reads=rd, writes=[out])
            else:
                S.op(eng, lambda e: e.tensor_scalar(out, in0, s1, s2, op0, op1), reads=rd, writes=[out])

        def stt(eng, out, in0, scalar, in1, op0, op1):
            rd = [in0, in1] + ([] if isinstance(scalar, (int, float)) else [scalar])
            S.op(eng, lambda e: e.scalar_tensor_tensor(out, in0, scalar, in1, op0, op1), reads=rd, writes=[out])

        def recip_lp(out, in_):
            def fn(e):
                with nc.allow_low_precision(reason="fp32 reciprocal, result stored as a bf16 matmul operand"):
                    return e.reciprocal(out, in_)
            S.op("dve", fn, reads=[in_], writes=[out])

        def memset(eng, ap, val):
            S.op(eng, lambda e: e.memset(ap, val), writes=[ap])

        def dma_sp(out, in_, semkey, reads=(), writes=(), group=None):
            S.dma("sp", lambda e: e.dma_start(out=out, in_=in_), reads=reads, writes=writes, semkey=semkey, group=group)

        cst = B.alloc([128, 896], F32)
        ident32 = cst[:, 0:128]
        causal32 = cst[:, 128:256]
        ucum32 = cst[:, 256:384]
        maskadd = cst[:, 512:768]
        neghalf = cst[:, 768:769]
        dma_sp(cst, consts_d, "ld_cst", writes=[cst])
        cbf = B.alloc([128, 256], BF16)
        ident16 = cbf[:, 0:128]
        ones16 = cbf[:, 128:256]
        vcopy("dve", ident16, ident32)
        vcopy("dve", ones16, cst[:, 384:512])
        gnw = B.alloc([128, 512], F32)
        dma_sp(gnw, norms_d[:, 3 * D:3 * D + 512], "ld_gnw", writes=[gnw])
        bgate = B.alloc([128, 32], F32)
        dma_sp(bgate, bgate_d, "ld_bg", writes=[bgate])
        small = B.alloc([128, 64], F32)
        small_i = [0]

        def stat_col():
            i = small_i[0] % 64
            small_i[0] += 1
            return small[:, i:i + 1]

        NSLOT = 3
        SLOT_BYTES = 16384
        PREFETCH = 2
        wslot_raw = [B.alloc([128, SLOT_BYTES // 2], BF16) for _ in range(NSLOT)]
        plan = plan_in if plan_in is not None else []
        issued = []
        cursor = [0]

        def _views(i, spec):
            si = i % NSLOT
            views = []
            o = 0
            for (wname, kc, c, w) in spec:
                views.append(wslot_raw[si][:, o:o + kc * w].rearrange("p (k n) -> p k n", n=w))
                o += kc * w
            assert o * 2 <= SLOT_BYTES
            return views

        def _issue(i):
            spec = plan[i]
            si = i % NSLOT
            views = _views(i, spec)
            grp = ("wf", i)
            for (wname, kc, c, w), view in zip(spec, views):
                wv = WD[wname].rearrange("(k p) n -> p k n", p=128)
                src = wv[:, :, c:c + w]
                S.dma("pool", (lambda e, dst=view, src=src: e.dma_start(out=dst, in_=src)),
                      writes=[view], semkey=("wslot", si), group=grp)
            issued.append(views)

        def take(spec):
            i = cursor[0]
            cursor[0] += 1
            rec.append(spec)
            if plan_in is None:
                return _views(i, spec)
            assert plan[i] == spec, (i, plan[i], spec)

            def ndesc(sp):
                return sum(kc * 8 for (_, kc, _, _) in sp)

            while len(issued) < min(len(plan), i + 1 + PREFETCH):
                j = len(issued)
                if j > i and sum(ndesc(plan[t]) for t in range(i, j + 1)) > 800:
                    break
                _issue(j)
            return issued[i]

        class BankPool:
            def __init__(self, ids):
                self.ids = list(ids)
                self.i = 0

            def one(self):
                b_ = self.ids[self.i % len(self.ids)]
                self.i += 1
                return b_

            def group(self, n):
                ng = len(self.ids) // n
                g_ = self.i % ng
                self.i += 1
                return self.ids[g_ * n]

        ALLB = BankPool(range(8))
        LOB = BankPool(range(4))
        HIB = BankPool(range(4, 8))

        def nbank():
            return ALLB.one()

        def ngroup(n):
            return ALLB.group(n)

        def g_gemm_ws(wview, kc, nm, rhs_fn, ntq, epilogue, m0=0, mrows=128, bp=None, split=False):
            bp = bp or ALLB
            for m in range(nm):
                if split:
                    passes = [[0, 1], [2, 3]]
                else:
                    passes = [list(range(ntq))]
                for tqs in passes:
                    base = bp.group(len(tqs) if len(tqs) in (2, 4) else 4)
                    pss = [None] * ntq
                    for i_, tq in enumerate(tqs):
                        pss[tq] = pbank(base + i_)[0:mrows, :]
                    for k in range(kc):
                        for tq in tqs:
                            mm(pss[tq], wview[:, k, m * 128:m * 128 + mrows], rhs_fn(k, tq), start=(k == 0), stop=(k == kc - 1))
                        yield
                    epilogue(m0 + m, pss)
                    yield

        def run(g):
            for _ in g:
                pass

        def gemm_ws(*a_, **kw):
            run(g_gemm_ws(*a_, **kw))

        def interleave(ga, gb, ra=1, rb=1):
            da = db = False
            while not (da and db):
                if not da:
                    for _ in range(ra):
                        try:
                            next(ga)
                        except StopIteration:
                            da = True
                            break
                if not db:
                    for _ in range(rb):
                        try:
                            next(gb)
                        except StopIteration:
                            db = True
                            break

        def limited(g, n):
            for _ in range(n):
                try:
                    next(g)
                except StopIteration:
                    return
                yield

        evr = [0]

        def ev_eng():
            evr[0] += 1
            return "act" if evr[0] % 2 else "dve"

        WD = {"w_in": w_in_d, "w_a": w_a_d, "w_b": w_b_d, "w_out": w_out_d, "w_f1": w_f1_d, "w_f2": w_f2_d}

        def spec_gate(p):
            return [("w_in", KC, C_GA + p * 256, 256)]

        def spec_lr():
            return [("w_in", KC, C_LR, 16)]

        def spec_qk(h):
            return [("w_in", KC, C_Q + h * 256, 256), ("w_in", KC, C_K + h * 256, 256)]

        def spec_vg(h, which):
            return [("w_in", KC, (C_V if which == 0 else C_G) + h * 512, 512)]

        def spec_dil(j, g):
            return [("w_in", KC, C_DIL + (3 * g + t_) * 1024 + j * 128, 128) for t_ in range(3)]

        def spec_ab(p):
            return [("w_a", KC, p * 256, 256), ("w_b", 8, p * 256, 256)]

        def spec_out(p):
            return [("w_out", KC, p * 512, 512)]

        def spec_f1(p):
            return [("w_f1", KC, p * 256, 256), ("w_f1", KC, DFF + p * 256, 256)]

        def spec_f2(m):
            return [("w_f2", FC, m * 128, 128)]

        hT_off = B.mark()
        hT = B.alloc([128, KC, T], BF16)
        phase_mark = B.mark()
        hT_end = phase_mark

        def rms_stat(rows, width):
            ss = stat_col()
            act(rms_junk[:, 0:width], rows, AF.Square, accum=ss)
            rs = stat_col()
            act(rs, ss, AF.Ln, bias=EPS, scale=1.0 / width)
            act(rs, rs, AF.Exp, scale=-0.5)
            return rs

        def rms_rows(rows, width, gmul, out):
            rs = rms_stat(rows, width)
            stt("dve", out, rows, rs, gmul, ALU.mult, ALU.mult)

        def xpose_rows(h_, dstT, tb, eng=None):
            for half in range(2):
                pv = pbank(nbank(), BF16)
                for c in range(8):
                    cc = half * 8 + c
                    tr(pv[:, c * 128:(c + 1) * 128], h_[:, cc * 128:(cc + 1) * 128], ident16)
                vcopy(eng or ev_eng(), dstT[:, half * 8:half * 8 + 8, tb * 128:(tb + 1) * 128],
                      pv.rearrange("p (c t) -> p c t", t=128))

        def row_pipeline(ntb, Lf, Tf, Rf, Xf):
            if Lf:
                Lf(0)
            for i in range(ntb + 2):
                if Lf and i + 1 < ntb:
                    Lf(i + 1)
                if Tf and i < ntb:
                    Tf(i)
                if Rf and 0 <= i - 1 < ntb:
                    Rf(i - 1)
                if Xf and 0 <= i - 2 < ntb:
                    Xf(i - 2)

        rms_junk = B.alloc([128, D], BF16)
        gbc = B.alloc([128, D], F32)
        dma_sp(gbc, norms_d[:, 0:D], "ld_gbc", writes=[gbc])
        NX = 6
        NH0 = 3
        xt = [B.alloc([128, D], F32) for _ in range(NX)]
        hn0 = [B.alloc([128, D], BF16) for _ in range(NH0)]

        def x_load(tb):
            t_ = xt[tb % NX]
            dma_sp(t_, x_d[tb * 128:(tb + 1) * 128, :], ("ld_x", tb % NX), writes=[t_])

        for tb0 in range(NX - 2):
            x_load(tb0)
        row_pipeline(T // 128, None, (lambda tb: x_load(tb + NX - 2) if tb + NX - 2 < T // 128 else None),
                     lambda tb: rms_rows(xt[tb % NX], D, gbc, hn0[tb % NH0]),
                     lambda tb: xpose_rows(hn0[tb % NH0], hT, tb))
        B.reset(phase_mark)

        def hT_rhs(k, tq):
            return hT[:, k, tq * 512:(tq + 1) * 512]

        G = None
        if stages >= 1:
            nbg = B.alloc([128, 32], F32)
            ts("dve", nbg, bgate, -1.0, None, ALU.mult)
            gt = [B.alloc([128, 1024], BF16) for _ in range(2)]
            gtmp = [B.alloc([128, 512], F32) for _ in range(2)]
            gi = [0]

            def gate_epi(m, pss):
                for hf in range(2):
                    if pss[hf * 2] is None:
                        continue
                    t_ = gt[gi[0] % 2]
                    for q2 in range(2):
                        tq = hf * 2 + q2
                        tm = gtmp[q2]
                        act(t_[:, q2 * 512:(q2 + 1) * 512], pss[tq], AF.Exp, bias=nbg[:, m:m + 1], scale=-1.0)
                    dma_sp(gates_d[m][:, hf * 1024:(hf + 1) * 1024], t_, ("st_gt", gi[0] % 2), reads=[t_],
                           writes=[("gates", m, hf)])
                    gi[0] += 1

            def g_gates():
                for p in range(16):
                    (wv_,) = take(spec_gate(p))
                    yield from g_gemm_ws(wv_, KC, 2, hT_rhs, 4, gate_epi, m0=p * 2, bp=LOB, split=True)

            G = g_gates()
            if stages < 2:
                run(G)
            phase_mark = B.mark()

        if stages >= 2:
            scale_k = 256 ** -0.5
            lrT = B.alloc([17, T], F32)
            memset("dve", lrT, 1.0)
            (wv,) = take(spec_lr())

            def lr_epi(m, pss):
                for tq in range(4):
                    vcopy(ev_eng(), lrT[0:16, tq * 512:(tq + 1) * 512], pss[tq])

            gemm_ws(wv, KC, 1, hT_rhs, 4, lr_epi, mrows=16)
            gla_mark = B.mark()

            for h in range(4):
                B.reset(gla_mark)
                wgk = B.alloc([17, 256], F32)
                dma_sp(wgk, wgk_d[:, h * 256:(h + 1) * 256], "ld_wgk", writes=[wgk])
                qeT = B.alloc([128, 2, T], BF16)
                keT = B.alloc([128, 2, T], BF16)
                v_h = B.alloc([128, 16, 512], BF16)
                bT_off = B.mark()
                bT = B.alloc([128, 2, T], F32)
                dec = B.alloc([128, 2, 16], F32)
                st32 = B.alloc([128, 2, 512], F32)
                st16 = B.alloc([128, 2, 512], BF16)
                tmp_mark = B.mark()
                B.reset(bT_off)
                sg_h = B.alloc([128, 16, 512], BF16)
                B.reset(tmp_mark)
                gkr = [B.alloc([128, 256], F32) for _ in range(2)]
                e1 = [B.alloc([128, 256], F32) for _ in range(2)]
                B.reset(tmp_mark)
                tmp512 = [B.alloc([128, 512], F32) for _ in range(2)]
                attb = [B.alloc([128, 128], BF16) for _ in range(2)]
                onb = [B.alloc([128, 512], BF16) for _ in range(2)]
                sdt = [B.alloc([128, 512], F32) for _ in range(2)]
                ket = [B.alloc([128, 256], BF16) for _ in range(2)]
                ogc = [B.alloc([128, 4, 128], BF16) for _ in range(2)]
                rms_junk = B.alloc([128, 512], BF16)

                def g_decay():
                    pp = BankPool([6, 7])
                    bkA, bkB = 4, 5

                    def stA(tb):
                        pu = pbank(pp.one(), F32, 256)
                        mm(pu, lrT[0:17, tb * 128:(tb + 1) * 128], wgk[0:17, :])
                        a1 = e1[tb % 2]
                        act(a1, pu, AF.Exp, scale=-1.0)
                        act(gkr[tb % 2], a1, AF.Ln, bias=1.0)

                    def stB(tb):
                        t4 = tb % 4
                        g1 = gkr[tb % 2]
                        for kc, bk in ((0, bkA), (1, bkB)):
                            mm(pbank(bk, F32, 128, t4 * 128), g1[:, kc * 128:(kc + 1) * 128], ucum32)
                        if t4 == 3:
                            tbg = tb // 4
                            vcopy("dve", bT[:, 0, tbg * 512:(tbg + 1) * 512], pbank(bkA))
                            vcopy("dve", bT[:, 1, tbg * 512:(tbg + 1) * 512], pbank(bkB))

                    stA(0)
                    yield
                    for tb in range(16):
                        if tb + 1 < 16:
                            stA(tb + 1)
                            yield
                        stB(tb)
                        yield

                if G is not None:
                    interleave(g_decay(), limited(G, 68), 1, 2)
                else:
                    run(g_decay())
                act(dec, bT.rearrange("p k (c t) -> p k c t", t=128)[:, :, :, 127], AF.Exp)

                wq, wk = take(spec_qk(h))
                ti = [0]

                def mk_qk_epi(isq):
                    def epi(m, pss):
                        kc = m
                        for tq in range(4):
                            tm = tmp512[ti[0] % 2]
                            ti[0] += 1
                            sl = slice(tq * 512, (tq + 1) * 512)
                            if isq:
                                act(tm, bT[:, kc, sl], AF.Exp)
                                stt("dve", qeT[:, kc, sl], pss[tq], scale_k, tm, ALU.mult, ALU.mult)
                            else:
                                act(tm, bT[:, kc, sl], AF.Exp, scale=-1.0)
                                tt("dve", keT[:, kc, sl], pss[tq], tm, ALU.mult)
                    return epi

                gemm_ws(wq, KC, 2, hT_rhs, 4, mk_qk_epi(True))
                gemm_ws(wk, KC, 2, hT_rhs, 4, mk_qk_epi(False))

                for which in range(2):
                    (wv,) = take(spec_vg(h, which))
                    for tbg in range(4):
                        base = ngroup(4)
                        for k in range(KC):
                            for t4 in range(4):
                                tb = tbg * 4 + t4
                                mm(pbank(base + t4), hT[:, k, tb * 128:(tb + 1) * 128], wv[:, k, :], start=(k == 0), stop=(k == KC - 1))
                        for t4 in range(4):
                            tb = tbg * 4 + t4
                            if which == 0:
                                vcopy(ev_eng(), v_h[:, tb, :], pbank(base + t4))
                            else:
                                tm = tmp512[ti[0] % 2]
                                ti[0] += 1
                                act(tm, pbank(base + t4), AF.Silu)
                                tt("dve", sg_h[:, tb, :], tm, gnw, ALU.mult)

                memset("dve", st32, 0.0)
                memset("dve", st16, 0.0)

                def g_rec(h=h, qeT=qeT, keT=keT, v_h=v_h, sg_h=sg_h, st32=st32, st16=st16, dec=dec, attb=attb, onb=onb,
                          sdt=sdt, ket=ket, ogc=ogc):
                    bS, bO, bK = 4, 5, (6, 7)

                    def att_mask(c):
                        csl = slice(c * 128, (c + 1) * 128)
                        pa = pbank(bS, F32, 128, 0)
                        for kc in range(2):
                            mm(pa, keT[:, kc, csl], qeT[:, kc, csl], start=(kc == 0), stop=(kc == 1))
                        tt("dve", attb[c % 2], pa, causal32, ALU.mult)

                    def k_T(c):
                        csl = slice(c * 128, (c + 1) * 128)
                        pvk = pbank(bS, BF16, 256, 256)
                        for kc in range(2):
                            tr(pvk[:, kc * 128:(kc + 1) * 128], keT[:, kc, csl], ident16)
                        vcopy("act", ket[c % 2], pvk)

                    def out_T(c):
                        csl = slice(c * 128, (c + 1) * 128)
                        ob = onb[c % 2]
                        pv = pbank(bS, BF16, 512, 512)
                        for j in range(4):
                            tr(pv[:, j * 128:(j + 1) * 128], ob[:, j * 128:(j + 1) * 128], ident16)
                        oc = ogc[c % 2]
                        vcopy("act", oc, pv.rearrange("p (j t) -> p j t", t=128))
                        dma_sp(oglaT_d[h * 4:(h + 1) * 4, :, csl].rearrange("c p t -> p c t"), oc, ("st_og", c % 2),
                               reads=[oc], writes=[("oglaT", h, c)])

                    att_mask(0)
                    k_T(0)
                    yield
                    for c in range(16):
                        csl = slice(c * 128, (c + 1) * 128)
                        po = pbank(bO)
                        mm(po, attb[c % 2], v_h[:, c, :], start=True, stop=False)
                        for kc in range(2):
                            mm(po, qeT[:, kc, csl], st16[:, kc, :], start=False, stop=(kc == 1))
                        rs = rms_stat(po, 512)
                        yield
                        if c < 15:
                            kt = ket[c % 2]
                            pks = []
                            for kc in range(2):
                                pk = pbank(bK[kc])
                                mm(pk, kt[:, kc * 128:(kc + 1) * 128], v_h[:, c, :])
                                pks.append(pk)
                            att_mask(c + 1)
                            for kc in range(2):
                                sd = sdt[kc]
                                act(sd, st32[:, kc, :], AF.Copy, scale=dec[:, kc, c:c + 1])
                                stt("dve", st16[:, kc, :], pks[kc], dec[:, kc, c:c + 1], sd, ALU.mult, ALU.add)
                            for kc in range(2):
                                stt("dve", st32[:, kc, :], pks[kc], dec[:, kc, c:c + 1], sdt[kc], ALU.mult, ALU.add)
                            yield
                        stt("dve", onb[c % 2], po, rs, sg_h[:, c, :], ALU.mult, ALU.mult)
                        if c + 1 < 15:
                            k_T(c + 1)
                        yield
                        if c >= 1:
                            out_T(c - 1)
                            yield
                    out_T(15)
                    yield

                if G is not None:
                    interleave(g_rec(), limited(G, 204), 1, 3)
                else:
                    run(g_rec())
            if G is not None:
                run(G)
            phase_mark = hT_end
            B.reset(phase_mark)

        if stages >= 3:
            scale_a = 128 ** -0.5
            B.reset(phase_mark)
            XT = [[B.alloc([128, T], BF16) for _ in range(3)] for _ in range(3)]
            vtok = [B.alloc([128, 16, 128], BF16) for _ in range(3)]
            acc = B.alloc([128, 2, T], F32)
            bm = B.alloc([128, 3, 256], F32)
            NR = 4
            stt_ = [B.alloc([128, 256], F32) for _ in range(NR)]
            ptl = [B.alloc([128, 256], BF16) for _ in range(NR)]
            odT = B.alloc([128, T], BF16)
            items = [(j, g) for j in range(8) for g in range(3)]

            def g_P(j, g):
                d = DILS[g]
                dma_sp(bm[:, g, :], btab_d[:, g * 8 + j, :], ("ld_bm", g), writes=[bm[:, g, :]])
                tt("dve", bm[:, g, :], bm[:, g, :], maskadd, ALU.add)
                wvs = take(spec_dil(j, g))
                for t_ in range(3):
                    def perm_epi(m, pss, g=g, d=d, t_=t_):
                        dst = XT[g][t_]
                        for tq in range(4):
                            if pss[tq] is None:
                                continue
                            if d == 1:
                                vcopy(ev_eng(), dst[:, tq * 512:(tq + 1) * 512], pss[tq])
                            else:
                                n = 512 // d
                                dv = dst.rearrange("p (r m) -> p r m", r=d)[:, :, tq * n:(tq + 1) * n]
                                vcopy(ev_eng(), dv, pss[tq].rearrange("p (m r) -> p r m", r=d))
                    yield from g_gemm_ws(wvs[t_], KC, 1, hT_rhs, 4, perm_epi, bp=LOB, split=True)
                for half in range(2):
                    pv = pbank(LOB.one(), BF16)
                    for b8 in range(8):
                        blk = half * 8 + b8
                        tr(pv[:, b8 * 128:(b8 + 1) * 128], XT[g][2][:, blk * 128:(blk + 1) * 128], ident16)
                    vcopy(ev_eng(), vtok[g][:, half * 8:(half + 1) * 8, :], pv.rearrange("p (b e) -> p b e", e=128))
                    yield

            uctr = [0]

            def g_A(j, g):
                d = DILS[g]
                L = T // d
                nblk = L // 128
                units = [(r, n) for r in range(d) for n in range(nblk)]

                def qk(u):
                    r, n = units[u]
                    blk = r * nblk + n
                    qb = XT[g][0][:, blk * 128:(blk + 1) * 128]
                    bks = HIB.one()
                    ps_s = pbank(bks, F32, 256)
                    parts = ([0] if n > 0 else []) + [1]
                    for x_ in parts:
                        kb = blk - 1 + x_
                        mm(ps_s[:, x_ * 128:(x_ + 1) * 128], XT[g][1][:, kb * 128:(kb + 1) * 128], qb)
                    lo = parts[0] * 128
                    sb = stt_[uctr[0] % NR]
                    pt = ptl[uctr[0] % NR]
                    uctr[0] += 1
                    stt("dve", sb[:, lo:256], ps_s[:, lo:256], scale_a, bm[:, g, lo:256], ALU.mult, ALU.add)
                    act(pt[:, lo:256], sb[:, lo:256], AF.Exp)
                    return (bks, parts, blk, pt, r, n)

                def pv_(ctx):
                    bks, parts, blk, pt, r, n = ctx
                    ps_n = pbank(bks, F32, 256, 256)
                    for ii, x_ in enumerate(parts):
                        kb = blk - 1 + x_
                        mm(ps_n[:, 0:128], vtok[g][:, kb, :], pt[:, x_ * 128:(x_ + 1) * 128],
                           start=(ii == 0), stop=(ii == len(parts) - 1))
                    for ii, x_ in enumerate(parts):
                        mm(ps_n[:, 128:256], ones16, pt[:, x_ * 128:(x_ + 1) * 128],
                           start=(ii == 0), stop=(ii == len(parts) - 1))
                    av = acc.rearrange("p x (m r) -> p x m r", r=d)[:, :, n * 128:(n + 1) * 128, r]
                    pn = ps_n.rearrange("p (x q) -> p x q", q=128)
                    if g == 0:
                        vcopy("act", av, pn)
                    else:
                        tt("dve", av, pn, av, ALU.add)

                LA = 2
                nu = len(units)
                ctxs = {}
                for u0 in range(min(LA, nu)):
                    ctxs[u0] = qk(u0)
                    yield
                for u in range(nu):
                    if u + LA < nu:
                        ctxs[u + LA] = qk(u + LA)
                        yield
                    pv_(ctxs.pop(u))
                    yield
                if g == 2:
                    act(acc[:, 1, :], acc[:, 1, :], AF.Ln)
                    act(acc[:, 1, :], acc[:, 1, :], AF.Exp, scale=-1.0)
                    tt("dve", odT, acc[:, 0, :], acc[:, 1, :], ALU.mult)
                    dma_sp(odilT_d[j], odT, "st_odT", reads=[odT], writes=[("odilT", j)])
                    yield

            run(g_P(*items[0]))
            for i, it in enumerate(items):
                if i + 1 < len(items):
                    interleave(g_A(*it), g_P(*items[i + 1]), 1, 3)
                else:
                    run(g_A(*it))
            B.reset(phase_mark)

        TH = T // 2
        post_mark = hT_off

        NRS = 3

        def rows_load(tb, res_d, r0, ybl, xrl):
            xr = xrl[tb % NRS]
            if ybl is not None:
                yb = ybl[tb % NRS]
                dma_sp(yb, y_d[:, :, tb * 128:(tb + 1) * 128].rearrange("c p t -> p c t"), ("ld_yb", tb % NRS),
                       reads=[("y", m) for m in range(16)], writes=[yb])
            dma_sp(xr, res_d[r0 + tb * 128:r0 + (tb + 1) * 128, :], ("ld_xr", tb % NRS),
                   reads=[("x1", r0 // TH, tb)] if res_d is x1_d else [], writes=[xr])

        def rows_add(tb, ybl, xrl, ysb=None):
            xr = xrl[tb % NRS]
            for cg in range(4):
                pv = pbank(nbank())
                for i in range(4):
                    if callable(ysb):
                        src = ysb(cg * 4 + i, tb)
                    elif ysb is not None:
                        src = ysb[:, cg * 4 + i, tb * 128:(tb + 1) * 128]
                    else:
                        src = ybl[tb % NRS][:, cg * 4 + i, :]
                    tr(pv[:, i * 128:(i + 1) * 128], src, ident32)
                tt("dve", xr[:, cg * 512:(cg + 1) * 512], pv, xr[:, cg * 512:(cg + 1) * 512], ALU.add)
            return xr

        if stages >= 4:
            for h2 in range(2):
                B.reset(post_mark)
                tsl = slice(h2 * TH, (h2 + 1) * TH)
                mgT = B.alloc([128, KC, TH], BF16)
                s1_mark = B.mark()
                ogh = B.alloc([128, 16, TH], BF16)
                odh = B.alloc([128, 8, TH], BF16)
                gab = [B.alloc([128, 2, TH], BF16) for _ in range(2)]
                gfl = [B.alloc([128, 2, TH], F32) for _ in range(2)]
                t1 = [B.alloc([128, 512], F32) for _ in range(2)]
                t2 = [B.alloc([128, 512], F32) for _ in range(2)]
                for i4 in range(4):
                    dma_sp(ogh[:, i4 * 4:(i4 + 1) * 4, :], oglaT_d[i4 * 4:(i4 + 1) * 4, :, tsl].rearrange("c p t -> p c t"),
                           ("ld_ogh", i4), reads=[("oglaT", i4, c) for c in range(16)], writes=[ogh[:, i4 * 4:(i4 + 1) * 4, :]])
                for i4 in range(2):
                    dma_sp(odh[:, i4 * 4:(i4 + 1) * 4, :], odilT_d[i4 * 4:(i4 + 1) * 4, :, tsl].rearrange("c p t -> p c t"),
                           ("ld_odh", i4), reads=[("odilT", i) for i in range(i4 * 4, (i4 + 1) * 4)],
                           writes=[odh[:, i4 * 4:(i4 + 1) * 4, :]])
                for p in range(8):
                    wa, wb = take(spec_ab(p))
                    for mi in range(2):
                        m = p * 2 + mi
                        gb_ = gab[m % 2]
                        dma_sp(gb_[:, 0, :], gates_d[m][:, tsl], ("ld_ga", m % 2), reads=[("gates", m, h2)], writes=[gb_[:, 0, :]])
                        dma_sp(gb_[:, 1, :], gates_d[16 + m][:, tsl], ("ld_gb", m % 2), reads=[("gates", 16 + m, h2)], writes=[gb_[:, 1, :]])
                        gf_ = gfl[m % 2]
                        act(gf_, gb_, AF.Ln, bias=1.0)
                        act(gf_, gf_, AF.Exp, scale=-1.0)
                        base = ngroup(4)
                        pA = [pbank(base + tq) for tq in range(2)]
                        pB = [pbank(base + 2 + tq) for tq in range(2)]
                        for k in range(KC):
                            for tq in range(2):
                                mm(pA[tq], wa[:, k, mi * 128:(mi + 1) * 128], ogh[:, k, tq * 512:(tq + 1) * 512],
                                   start=(k == 0), stop=(k == KC - 1))
                        for k in range(8):
                            for tq in range(2):
                                mm(pB[tq], wb[:, k, mi * 128:(mi + 1) * 128], odh[:, k, tq * 512:(tq + 1) * 512],
                                   start=(k == 0), stop=(k == 7))
                        for tq in range(2):
                            sl = slice(tq * 512, (tq + 1) * 512)
                            tt("dve", t1[tq], pA[tq], gf_[:, 0, sl], ALU.mult)
                            tt("dve", t2[tq], pB[tq], gf_[:, 1, sl], ALU.mult)
                            tt("dve", mgT[:, m, sl], t1[tq], t2[tq], ALU.add)
                B.reset(s1_mark)
                yall = B.alloc([128, 16, TH], F32)

                def y_sb_epi(m, pss):
                    for tq in range(2):
                        vcopy(ev_eng(), yall[:, m, tq * 512:(tq + 1) * 512], pss[tq])

                yT = None
                yi = [0]

                def y_epi(m, pss):
                    y = yT[yi[0] % 2]
                    for tq in range(2):
                        vcopy(ev_eng(), y[:, tq * 512:(tq + 1) * 512], pss[tq])
                    dma_sp(y_d[m], y, ("st_y", yi[0] % 2), reads=[y], writes=[("y", m)])
                    yi[0] += 1

                for p in range(4):
                    (wo,) = take(spec_out(p))
                    gemm_ws(wo, KC, 4, lambda k, tq: mgT[:, k, tq * 512:(tq + 1) * 512], 2, y_sb_epi, m0=p * 4)
                s2b_mark = B.mark()
                B.reset(post_mark)
                hfT = B.alloc([128, KC, TH], BF16)
                B.reset(s2b_mark)
                xrl = [B.alloc([128, D], F32) for _ in range(NRS)]
                gbc2 = B.alloc([128, D], F32)
                hn2 = [B.alloc([128, D], BF16) for _ in range(2)]
                rms_junk = B.alloc([128, D], BF16)
                dma_sp(gbc2, norms_d[:, D:2 * D], "ld_gbc2", writes=[gbc2])

                def s2b_T(tb):
                    xr = rows_add(tb, None, xrl, ysb=yall)
                    dma_sp(x1_d[h2 * TH + tb * 128:h2 * TH + (tb + 1) * 128, :], xr, ("st_x1", tb % NRS), reads=[xr],
                           writes=[("x1", h2, tb)])

                row_pipeline(8, lambda tb: rows_load(tb, x_d, h2 * TH, None, xrl), s2b_T,
                             (lambda tb: rms_rows(xrl[tb % NRS], D, gbc2, hn2[tb % 2])) if stages >= 5 else None,
                             (lambda tb: xpose_rows(hn2[tb % 2], hfT, tb, eng="act")) if stages >= 5 else None)
                if stages >= 5:
                    B.reset(s1_mark)
                    actT = B.alloc([128, FC, TH], BF16)
                    act_end = B.mark()
                    ft = [B.alloc([128, 512], F32) for _ in range(2)]
                    fi = [0]
                    for p in range(22):
                        wg_, wu_ = take(spec_f1(p))
                        for fi_ in range(2):
                            f = p * 2 + fi_
                            base = ngroup(4)
                            pG = [pbank(base + tq) for tq in range(2)]
                            pU = [pbank(base + 2 + tq) for tq in range(2)]
                            for k in range(KC):
                                for tq in range(2):
                                    mm(pG[tq], wg_[:, k, fi_ * 128:(fi_ + 1) * 128], hfT[:, k, tq * 512:(tq + 1) * 512],
                                       start=(k == 0), stop=(k == KC - 1))
                            for k in range(KC):
                                for tq in range(2):
                                    mm(pU[tq], wu_[:, k, fi_ * 128:(fi_ + 1) * 128], hfT[:, k, tq * 512:(tq + 1) * 512],
                                       start=(k == 0), stop=(k == KC - 1))
                            for tq in range(2):
                                tm = ft[fi[0] % 2]
                                fi[0] += 1
                                act(tm, pG[tq], AF.Silu)
                                tt("dve", actT[:, f, tq * 512:(tq + 1) * 512], tm, pU[tq], ALU.mult)
                    B.reset(post_mark)
                    yA = B.alloc([128, 8, TH], F32)
                    B.reset(act_end)
                    yB = B.alloc([128, 8, TH], F32)

                    def y2_epi(m, pss):
                        dst = yA if m < 8 else yB
                        for tq in range(2):
                            vcopy(ev_eng(), dst[:, m % 8, tq * 512:(tq + 1) * 512], pss[tq])

                    for m in range(16):
                        (wv,) = take(spec_f2(m))
                        gemm_ws(wv, FC, 1, lambda k, tq: actT[:, k, tq * 512:(tq + 1) * 512], 2, y2_epi, m0=m)
                    B.reset(s1_mark)
                    xrl = [B.alloc([128, D], F32) for _ in range(NRS)]
                    gbc3 = B.alloc([128, D], F32)
                    rms_junk = B.alloc([128, D], BF16)
                    orow = [B.alloc([128, D], F32) for _ in range(2)]
                    assert B.mark() <= act_end
                    dma_sp(gbc3, norms_d[:, 2 * D:3 * D], "ld_gbc3", writes=[gbc3])

                    def y2_src(c, tb):
                        return (yA if c < 8 else yB)[:, c % 8, tb * 128:(tb + 1) * 128]

                    def s4b_R(tb):
                        o_ = orow[tb % 2]
                        rms_rows(xrl[tb % NRS], D, gbc3, o_)
                        r0 = h2 * TH + tb * 128
                        dma_sp(out_d[r0:r0 + 128, :], o_, ("st_out", tb % 2), reads=[o_], writes=[("out", h2, tb)])

                    row_pipeline(8, lambda tb: rows_load(tb, x1_d, h2 * TH, None, xrl),
                                 lambda tb: rows_add(tb, None, xrl, ysb=y2_src), s4b_R, None)

        final_reads = []
        for k in list(S.last_writer.keys()):
            if isinstance(k, tuple) and k[0] in ("out", "x1", "oglaT", "odilT", "gates", "y"):
                final_reads.append(k)
        S.op("sp", None, reads=final_reads)
        assert plan_in is None or cursor[0] == len(plan), (cursor[0], len(plan))
        S.emit(st)
        print(f"[build] ops={len(S.ops)} sems={S.nsem}")
    return nc


def _t5_bucket(dist):
    max_exact = 16
    d = np.maximum(dist, 1).astype(np.float64)
    large = max_exact + (np.log(d / max_exact) / math.log(2048 / max_exact) * (32 - max_exact)).astype(np.int64)
    large = np.minimum(large, 31)
    return np.where(dist < max_exact, dist, large).astype(np.int64)


def _host_consts():
    c = np.zeros((128, 896), np.float32)
    j = np.arange(128)[:, None]
    i = np.arange(128)[None, :]
    c[:, 0:128] = np.eye(128, dtype=np.float32)
    c[:, 128:256] = (j <= i)
    c[:, 256:384] = np.where(j <= i, -1.0 / 16.0, 0.0)
    c[:, 384:512] = 1.0
    c[:, 512:640] = np.where(j >= i, 0.0, NEG)
    c[:, 640:768] = np.where(j <= i, 0.0, NEG)
    c[:, 768:896] = -0.5
    return c


def _bias_index():
    cidx = np.arange(128)[:, None]
    a = np.arange(128)[None, :]
    idx = np.zeros((3, 128, 2, 128), np.int64)
    for g, d in enumerate(DILS):
        steps_prev = 128 + a - cidx
        steps_cur = a - cidx
        idx[g, :, 0, :] = _t5_bucket(np.clip(steps_prev, 0, None) * d)
        idx[g, :, 1, :] = _t5_bucket(np.clip(steps_cur, 0, None) * d)
    return idx


_NC_CACHE = {}


def make_in_maps(inputs):
    f = lambda a: np.ascontiguousarray(np.asarray(a, dtype=np.float32))
    x = f(inputs["x"])
    norms = np.concatenate([f(inputs["attn_norm"])[0], f(inputs["ffn_norm"])[0], f(inputs["final_norm"]),
                            f(inputs["gla_norm"])[0]])
    norms = np.ascontiguousarray(np.broadcast_to(norms[None, :], (128, norms.shape[0])))
    wgk = np.ascontiguousarray(np.concatenate([f(inputs["w_gk_up"])[0], f(inputs["b_gk"])[0][None, :]], axis=0))
    bgate = np.ascontiguousarray(f(inputs["b_gate"])[0].reshape(32, 128).T)
    rb = f(inputs["rel_bias"])
    idx = _bias_index()
    btab = np.zeros((128, 24, 256), np.float32)
    for g in range(3):
        for s in range(8):
            hh = g * 8 + s
            btab[:, hh, :] = rb[:, hh][idx[g]].reshape(128, 256)
    shared = {
        "w_in": f(inputs["w_in"])[0], "w_a": f(inputs["w_branch_gla"])[0], "w_b": f(inputs["w_branch_dil"])[0],
        "w_out": f(inputs["w_out"])[0], "w_f1": f(inputs["w_ffn_in"])[0], "w_f2": f(inputs["w_ffn_out"])[0],
        "norms": norms, "consts": _host_consts(), "wgk": wgk, "bgate": bgate, "btab": btab,
    }
    return [dict(shared, x=np.ascontiguousarray(x[b])) for b in range(x.shape[0])]


def kernel(**inputs):
    in_maps = make_in_maps(inputs)
    if "nc" not in _NC_CACHE:
        _NC_CACHE["nc"] = build()
    nc = _NC_CACHE["nc"]
    res = run_bass_kernel_spmd(nc, in_maps, core_ids=list(range(len(in_maps))))
    out = np.stack([np.asarray(r["out"], dtype=np.float32) for r in res.results], axis=0)
    return out
```
